# Optimizing a Trainium2 kernel written in Bass

```python
import jax, jax.numpy as jnp
from jax import lax
import numpy as np

D_MODEL = 1024
BATCH = 8
SEQ = 2048
DEPTH = 1
DEC_BATCH = 32
DEC_SEQ = 4
PAST_LEN = 16384
PAGE_SIZE = 128

D_MLA = D_MODEL // 2
D_CONV = D_MODEL - D_MLA
N_HEADS = 8
QK_NOPE = 64
QK_ROPE = 32
V_HEAD = D_MLA // N_HEADS
Q_LORA = 384
KV_LORA = 256
CONV_W = 3
ROPE_THETA = 10000.0
EPS = 1e-6
Q_BLOCK = 128
ATTN_SCALE = (QK_NOPE + QK_ROPE) ** -0.5
IN_SIZES = (Q_LORA, KV_LORA, QK_ROPE, D_MLA, D_CONV, D_CONV, D_CONV, D_CONV)
D_IN = Q_LORA + KV_LORA + QK_ROPE + D_MLA + 4 * D_CONV

kernel_name = "hymba_mla_shortconv_decode_step"


def rmsnorm(x, g):
    xf = x.astype(jnp.float32)
    y = xf * lax.rsqrt(jnp.mean(xf * xf, axis=-1, keepdims=True) + EPS) * g.astype(jnp.float32)
    return y.astype(x.dtype)


def rope(x, pos):
    r = x.shape[-1]
    inv_freq = ROPE_THETA ** (-jnp.arange(0, r, 2, dtype=jnp.float32) / r)
    ang = pos.astype(jnp.float32)[:, None] * inv_freq[None, :]
    ang = jnp.concatenate([ang, ang], axis=-1)
    shape = (1, x.shape[1]) + (1,) * (x.ndim - 3) + (r,)
    cos = jnp.cos(ang).reshape(shape)
    sin = jnp.sin(ang).reshape(shape)
    xf = x.astype(jnp.float32)
    x1, x2 = xf[..., : r // 2], xf[..., r // 2:]
    rot = jnp.concatenate([-x2, x1], axis=-1)
    return (xf * cos + rot * sin).astype(x.dtype)


def split_points():
    pts, acc = [], 0
    for s in IN_SIZES[:-1]:
        acc += s
        pts.append(acc)
    return pts


def branch_inputs(xn, pos, w_in, g_qnorm, w_uq, g_kvnorm, w_ukv):
    b, t = xn.shape[0], xn.shape[1]
    z = xn @ w_in
    c_q, c_kv, k_pe, g_mla, b_gate, c_gate, h, g_conv = jnp.split(z, split_points(), axis=-1)
    q = (rmsnorm(c_q, g_qnorm) @ w_uq).reshape(b, t, N_HEADS, QK_NOPE + QK_ROPE)
    q_nope, q_pe = q[..., :QK_NOPE], rope(q[..., QK_NOPE:], pos)
    ckv = rmsnorm(c_kv, g_kvnorm)
    kpe = rope(k_pe, pos)
    w_uk = w_ukv.reshape(KV_LORA, N_HEADS, QK_NOPE + V_HEAD)[..., :QK_NOPE]
    q_lat = jnp.einsum('bthn,chn->bthc', q_nope, w_uk)
    u = c_gate * h
    return q_lat, q_pe, ckv, kpe, g_mla, b_gate, u, g_conv


def mla_attend(q_lat, q_pe, ckv, kpe, q_pos, k_pos):
    s = jnp.einsum('bqhc,bkc->bhqk', q_lat, ckv) + jnp.einsum('bqhr,bkr->bhqk', q_pe, kpe)
    s = s.astype(jnp.float32) * ATTN_SCALE
    mask = k_pos[None, :] <= q_pos[:, None]
    s = jnp.where(mask[None, None], s, -jnp.inf)
    p = jax.nn.softmax(s, axis=-1).astype(ckv.dtype)
    return jnp.einsum('bhqk,bkc->bqhc', p, ckv)


def prompt_attention(q_lat, q_pe, ckv, kpe, pos):
    b, s = q_lat.shape[0], q_lat.shape[1]
    nb = s // Q_BLOCK
    qb = q_lat.reshape(b, nb, Q_BLOCK, N_HEADS, KV_LORA).transpose(1, 0, 2, 3, 4)
    pb = q_pe.reshape(b, nb, Q_BLOCK, N_HEADS, QK_ROPE).transpose(1, 0, 2, 3, 4)
    posb = pos.reshape(nb, Q_BLOCK)
    o = lax.map(lambda a: mla_attend(a[0], a[1], ckv, kpe, a[2], pos), (qb, pb, posb))
    return o.transpose(1, 0, 2, 3, 4).reshape(b, s, N_HEADS, KV_LORA)


def short_conv(u_pad, w_conv):
    t = u_pad.shape[1] - (CONV_W - 1)
    y = w_conv[0] * u_pad[:, 0:t]
    for k in range(1, CONV_W):
        y = y + w_conv[k] * u_pad[:, k:k + t]
    return y


def branch_output(o_lat, g_mla, y_conv, b_gate, g_conv, w_ukv, w_out, g_post):
    b, t = o_lat.shape[0], o_lat.shape[1]
    w_uv = w_ukv.reshape(KV_LORA, N_HEADS, QK_NOPE + V_HEAD)[..., QK_NOPE:]
    o = jnp.einsum('bthc,chv->bthv', o_lat, w_uv).reshape(b, t, D_MLA)
    mix = jnp.concatenate([o * jax.nn.silu(g_mla), b_gate * y_conv * jax.nn.silu(g_conv)], axis=-1)
    return rmsnorm(mix @ w_out, g_post)


def setup_inputs(seed: int = 0) -> dict:
    key = jax.random.key(seed)
    ks = jax.random.split(key, 20)
    n_pages = PAST_LEN // PAGE_SIZE
    n_used = DEC_BATCH * n_pages
    n_phys = n_used + n_used // 4
    f32 = jnp.float32

    def w(k, shape, fan_in):
        return jax.random.normal(k, shape, f32) * fan_in ** -0.5

    def gain(k, d):
        return 1.0 + 0.02 * jax.random.normal(k, (DEPTH, d), f32)

    page_table = jax.random.permutation(ks[0], n_phys)[:n_used].reshape(DEC_BATCH, n_pages).astype(jnp.int32)
    return {
        "x_prompt": jax.random.normal(ks[1], (BATCH, SEQ, D_MODEL), f32),
        "x_sample": jax.random.normal(ks[2], (DEC_BATCH, DEC_SEQ, D_MODEL), f32),
        "cache_ckv": jax.random.normal(ks[3], (DEPTH, n_phys, PAGE_SIZE, KV_LORA), f32),
        "cache_kpe": jax.random.normal(ks[4], (DEPTH, n_phys, PAGE_SIZE, QK_ROPE), f32),
        "state_conv": jax.random.normal(ks[5], (DEPTH, DEC_BATCH, CONV_W - 1, D_CONV), f32),
        "page_table": page_table,
        "g_pre": gain(ks[6], D_MODEL),
        "w_in": w(ks[7], (DEPTH, D_MODEL, D_IN), D_MODEL),
        "g_qnorm": gain(ks[8], Q_LORA),
        "w_uq": w(ks[9], (DEPTH, Q_LORA, N_HEADS * (QK_NOPE + QK_ROPE)), Q_LORA),
        "g_kvnorm": gain(ks[10], KV_LORA),
        "w_ukv": w(ks[11], (DEPTH, KV_LORA, N_HEADS * (QK_NOPE + V_HEAD)), KV_LORA),
        "w_conv": w(ks[12], (DEPTH, CONV_W, D_CONV), CONV_W),
        "w_out": w(ks[13], (DEPTH, D_MODEL, D_MODEL), D_MODEL),
        "g_post": gain(ks[14], D_MODEL),
    }


def reference(x_prompt, x_sample, cache_ckv, cache_kpe, state_conv, page_table,
              g_pre, w_in, g_qnorm, w_uq, g_kvnorm, w_ukv, w_conv, w_out, g_post):
    pos_p = jnp.arange(SEQ, dtype=jnp.int32)
    pos_s = PAST_LEN + jnp.arange(DEC_SEQ, dtype=jnp.int32)
    k_pos_s = jnp.arange(PAST_LEN + DEC_SEQ, dtype=jnp.int32)
    xp, xs = x_prompt, x_sample
    ckv_p_l, kpe_p_l, conv_p_l, ckv_s_l, kpe_s_l, conv_s_l = [], [], [], [], [], []
    for l in range(DEPTH):
        q_lat, q_pe, ckv, kpe, g_mla, b_gate, u, g_conv = branch_inputs(
            rmsnorm(xp, g_pre[l]), pos_p, w_in[l], g_qnorm[l], w_uq[l], g_kvnorm[l], w_ukv[l])
        o_lat = prompt_attention(q_lat, q_pe, ckv, kpe, pos_p)
        u_pad = jnp.concatenate([jnp.zeros((u.shape[0], CONV_W - 1, D_CONV), u.dtype), u], axis=1)
        y_conv = short_conv(u_pad, w_conv[l])
        xp = xp + branch_output(o_lat, g_mla, y_conv, b_gate, g_conv, w_ukv[l], w_out[l], g_post[l])
        ckv_p_l.append(ckv)
        kpe_p_l.append(kpe)
        conv_p_l.append(u_pad[:, -(CONV_W - 1):])
        q_lat, q_pe, ckv, kpe, g_mla, b_gate, u, g_conv = branch_inputs(
            rmsnorm(xs, g_pre[l]), pos_s, w_in[l], g_qnorm[l], w_uq[l], g_kvnorm[l], w_ukv[l])
        past_ckv = cache_ckv[l][page_table].reshape(DEC_BATCH, PAST_LEN, KV_LORA)
        past_kpe = cache_kpe[l][page_table].reshape(DEC_BATCH, PAST_LEN, QK_ROPE)
        keys_ckv = jnp.concatenate([past_ckv, ckv], axis=1)
        keys_kpe = jnp.concatenate([past_kpe, kpe], axis=1)
        o_lat = mla_attend(q_lat, q_pe, keys_ckv, keys_kpe, pos_s, k_pos_s)
        u_pad = jnp.concatenate([state_conv[l].astype(u.dtype), u], axis=1)
        y_conv = short_conv(u_pad, w_conv[l])
        xs = xs + branch_output(o_lat, g_mla, y_conv, b_gate, g_conv, w_ukv[l], w_out[l], g_post[l])
        ckv_s_l.append(ckv)
        kpe_s_l.append(kpe)
        conv_s_l.append(u_pad[:, -(CONV_W - 1):])
    new_ckv_prompt = jnp.stack(ckv_p_l)
    new_kpe_prompt = jnp.stack(kpe_p_l)
    new_conv_prompt = jnp.stack(conv_p_l)
    new_ckv_sample = jnp.stack(ckv_s_l)
    new_kpe_sample = jnp.stack(kpe_s_l)
    new_conv_sample = jnp.stack(conv_s_l)
    return (xp, xs, new_ckv_prompt, new_kpe_prompt, new_conv_prompt, new_ckv_sample, new_kpe_sample, new_conv_sample)
```

```python
import numpy as np
from contextlib import ExitStack
import concourse.bass as bass
import concourse.mybir as mybir
from concourse.bass_utils import run_bass_kernel_spmd

F32 = mybir.dt.float32
BF16 = mybir.dt.bfloat16
I32 = mybir.dt.int32
AF = mybir.ActivationFunctionType
ALU = mybir.AluOpType

D = 1024
SEQ = 2048
NTT = 16
DIN = 3232
PAST = 16384
NPHYS = 5120
EPS = 1e-6
SCALE = 96 ** -0.5
NEG = -30000.0


class Ctx:
    def __init__(self, nc, stack, same_engine_sync=True):
        self.nc = nc
        self.stack = stack
        self.E = {'pe': nc.tensor, 'act': nc.scalar, 'dve': nc.vector, 'pool': nc.gpsimd, 'sp': nc.sync}
        self.sem = {}
        self.cnt = {}
        for e in self.E:
            self.sem[e] = stack.enter_context(nc.semaphore("s_" + e))
            self.cnt[e] = 0
        self.seen = {e: {} for e in self.E}
        self.res = {}
        self.dsem = {}
        self.dcnt = {}
        self.same = same_engine_sync
        self.nwaits = 0

    def _r(self, name):
        if name not in self.res:
            self.res[name] = {'w': None, 'r': {}}
        return self.res[name]

    def _deps(self, reads, writes):
        deps = []
        for r in reads:
            w = self._r(r)['w']
            if w is not None:
                deps.append(w)
        for wn in writes:
            rr = self._r(wn)
            if rr['w'] is not None:
                deps.append(rr['w'])
            deps.extend(rr['r'].items())
        return deps

    def _wait(self, e, deps):
        eng = self.E[e]
        best = {}
        for (key, val) in deps:
            if key == e and (e == 'pe' or not self.same):
                continue
            if val > best.get(key, 0):
                best[key] = val
        for key, val in best.items():
            if self.seen[e].get(key, 0) >= val:
                continue
            s = self.sem[key] if key in self.sem else self.dsem[key]
            eng.wait_ge(s, val)
            self.nwaits += 1
            self.seen[e][key] = val

    def _record(self, tok, reads, writes):
        for r in reads:
            d = self._r(r)['r']
            if tok[1] > d.get(tok[0], 0):
                d[tok[0]] = tok[1]
        for wn in writes:
            self.res[wn] = {'w': tok, 'r': {}}

    def op(self, e, fn, reads=(), writes=(), signal=True):
        self._wait(e, self._deps(reads, writes))
        inst = fn(self.E[e])
        if signal:
            self.cnt[e] += 1
            inst.then_inc(self.sem[e], 1)
            tok = (e, self.cnt[e])
        else:
            tok = (e, self.cnt[e] + 1)
        self._record(tok, reads, writes)
        return inst

    def dma(self, q, out, in_, reads=(), writes=(), key=None, indirect=None):
        if key is None:
            key = 'd:' + (writes[0] if writes else reads[0])
        if key not in self.dsem:
            self.dsem[key] = self.stack.enter_context(self.nc.semaphore("s_" + key.replace(':', '_')))
            self.dcnt[key] = 0
        self._wait(q, self._deps(reads, writes))
        if indirect is None:
            inst = self.E[q].dma_start(out=out, in_=in_)
        else:
            inst = self.E[q].indirect_dma_start(out=out, out_offset=None, in_=in_, in_offset=indirect)
        self.dcnt[key] += 16
        inst.then_inc(self.dsem[key], 16)
        self._record((key, self.dcnt[key]), reads, writes)
        return inst

    def finish(self, e='sp'):
        deps = [(k, v) for k, v in self.dcnt.items()]
        for k in self.E:
            if k != e and self.cnt[k] > 0:
                deps.append((k, self.cnt[k]))
        self._wait(e, deps)


def build_nc():
    nc = bass.Bass("TRN2", target_bir_lowering=False)

    def din(name, shape, dt=F32):
        return nc.dram_tensor(name, list(shape), dt, kind="ExternalInput").ap()

    def dout(name, shape, dt=F32):
        return nc.dram_tensor(name, list(shape), dt, kind="ExternalOutput").ap()

    xp = din("xp", [SEQ, D])
    xs = din("xs", [16, D])
    cckv = din("cckv", [NPHYS * 16, 2048])
    ckpe = din("ckpe", [NPHYS * 2, 2048])
    sconv = din("sconv", [128, 4, 4, 2])
    ptab = din("ptab", [128, 4], I32)
    gpre = din("gpre", [128, 8])
    w_in = din("w_in", [D, DIN])
    gq = din("gq", [128, 3])
    w_uq = din("w_uq", [384, 768])
    gkv = din("gkv", [128, 256])
    w_ukv = din("w_ukv", [256, 1024])
    wconv = din("wconv", [128, 4, 3])
    w_out = din("w_out", [D, D])
    gpost = din("gpost", [128, D])
    r_cos = din("r_cos", [128, NTT, 32])
    r_sinm = din("r_sinm", [128, NTT, 32])
    r_ct = din("r_ct", [96, SEQ])
    r_st = din("r_st", [96, SEQ])
    rs_cos = din("rs_cos", [16, 32])
    rs_sinm = din("rs_sinm", [16, 32])
    rs_ct = din("rs_ct", [32, 16])
    rs_st = din("rs_st", [32, 16])
    smask = din("smask", [16, 4, 32])

    y_p = dout("y_p", [SEQ, D])
    y_s = dout("y_s", [16, D])
    ckv_p = dout("ckv_p", [SEQ, 256])
    kpe_p = dout("kpe_p", [SEQ, 32])
    conv_p = dout("conv_p", [128, 4, 2])
    ckv_s = dout("ckv_s", [16, 256])
    kpe_s = dout("kpe_s", [16, 32])
    conv_s = dout("conv_s", [128, 4, 4, 2])

    with ExitStack() as st:
        cx = Ctx(nc, st)

        def T(name, shape, dt):
            return st.enter_context(nc.sbuf_tensor(name, list(shape), dt))

        pb = [st.enter_context(nc.psum_tensor("pb%d" % i, [128, 512], F32)) for i in range(8)]

        wbig = T("wbig", [128, 8 * DIN], BF16)
        w_in_b = wbig[:, :].rearrange("p (k n) -> p k n", k=8)
        ident_f = T("ident_f", [128, 128], F32)
        ident_b = T("ident_b", [128, 128], BF16)
        maskneg = T("maskneg", [128, 128], BF16)
        ones_b = T("ones_b", [128, 2], BF16)
        eps_t = T("eps_t", [128, 1], F32)
        gpre_c = T("gpre_c", [128, 8], F32)
        gq_c = T("gq_c", [128, 3], F32)
        gkv_b = T("gkv_b", [128, 256], F32)
        wconv_c = T("wconv_c", [128, 4, 3], F32)
        w_uq_b = T("w_uq_b", [128, 3, 800], BF16)
        w_uqr_b = T("w_uqr_b", [128, 3, 8, 96], BF16)
        w_ukv_b = T("w_ukv_b", [128, 2, 1024], BF16)
        w_ukT_b = T("w_ukT_b", [64, 8, 256], BF16)
        w_uvp_b = T("w_uvp_b", [128, 2, 8, 128], BF16)
        cos_t = T("cos_t", [128, NTT, 32], F32)
        sinm_t = T("sinm_t", [128, NTT, 32], F32)
        xst = [T("xst%d" % i, [128, D], F32) for i in range(2)]
        sqs = T("sqs", [128, D], BF16)
        stat = T("stat", [128, 384], F32)
        xT_b = T("xT_b", [128, 8, 512], BF16)
        cqT_b = T("cqT_b", [128, 3, SEQ], BF16)
        ckvT_b = T("ckvT_b", [128, 2, SEQ], BF16)
        kpeT_b = T("kpeT_b", [96, SEQ], BF16)
        mixT = T("mixT", [128, 8, SEQ], BF16)
        uT = [T("uT%d" % d, [128, 514], F32) for d in range(4)]
        tmpA = T("tmpA", [128, 512], F32)
        tmpB = T("tmpB", [128, 512], F32)
        tmpC = T("tmpC", [128, 512], F32)
        tmpD = T("tmpD", [128, 512], F32)
        cqn = T("cqn", [128, 384], F32)
        ckvn = [T("ckvn%d" % i, [128, 256], F32) for i in range(2)]
        kst1 = T("kst", [128, 96], F32)
        kst = [kst1, kst1]
        xTs = T("xTs", [128, 8, 16], BF16)
        cqTs = T("cqTs", [128, 3, 16], BF16)
        ckvTs = T("ckvTs", [128, 2, 16], BF16)
        kpeTs = T("kpeTs", [32, 16], BF16)
        ckvn_sb = T("ckvn_sb", [16, 256], BF16)
        mixTs = T("mixTs", [128, 8, 16], BF16)
        uS = T("uS", [128, 4, 4, 6], F32)
        sm_t = T("sm_t", [16, 4, 32], F32)
        cos_s = T("cos_s", [16, 32], F32)
        sinm_s = T("sinm_s", [16, 32], F32)
        ct_s = T("ct_s", [32, 16], F32)
        st_s = T("st_s", [32, 16], F32)
        qlatT = T("qlatT", [128, 2, 4, 8, 4], BF16)
        qpeT = T("qpeT", [32, 4, 8, 4], BF16)
        qnT_s = T("qnT_s", [64, 16], BF16)
        olatT = T("olatT", [128, 2, 8, 16], BF16)
        idx_t = T("idx_t", [128, 4], I32)
        idx_c = T("idx_c", [128, 4, 16], I32)
        idx_k = T("idx_k", [128, 4, 2], I32)
        gck = [T("gck%d" % i, [128, 2048], BF16) for i in range(3)]
        gkp = T("gkp", [128, 2048], BF16)
        ckTs = [T("ckTs%d" % i, [128, 2, 8, 128], BF16) for i in range(2)]
        kpTs = [T("kpTs%d" % i, [32, 8, 128], BF16) for i in range(2)]
        pTs = [T("pTs%d" % i, [128, 256], BF16) for i in range(2)]
        pTn = T("pTn", [16, 32], BF16)
        ssm = T("ssm", [32, 8], F32)
        cpst = T("cpst", [128, 4, 2], F32)

        LA = {}
        off = [0]

        def late(name, parts, nelem_bf16):
            a = off[0]
            off[0] += nelem_bf16
            assert off[0] <= 8 * DIN
            LA[name] = wbig[0:parts, a:a + nelem_bf16]
            return LA[name]

        w_out_b = late("w_out", 128, 8 * D).rearrange("p (k n) -> p k n", k=8)
        QT = late("QT", 128, SEQ)
        KT = late("KT", 128, SEQ)
        Vh = late("Vh", 128, 16 * 65 + 16)[:, 0:16 * 65].rearrange("p (k v) -> p k v", v=65)
        pT = [late("pT%d" % i, 128, 512) for i in range(3)]
        opair = late("opair", 128, 2 * 16 * 128).bitcast(F32).rearrange("p (q v) -> p q v", v=128)
        gpost_b = late("gpost", 128, 2 * D).bitcast(F32)
        ysb = late("ysb", 128, 2 * 1024).bitcast(F32)
        ct_blk = late("ct_blk", 96, 1024).bitcast(F32)
        st_blk = late("st_blk", 96, 1024).bitcast(F32)
        olat = tmpD[0:32, 256:512]
        LATE = 'wbig'
        xTflat = xT_b[:, :, :].rearrange("p k n -> p (k n)")
        QT1 = xTflat[:, 0:SEQ]
        Vh1 = xTflat[:, SEQ:SEQ + 16 * 65].rearrange("p (k v) -> p k v", v=65)
        KT1 = xst[0][:, :].bitcast(BF16)
        wout_state = {}
        sample_done = [False]
        phase_now = [1]

        zeros260 = w_uqr_b[:, 0, 0:5, 0:52]
        def mm(out, lhsT, rhs, start, stop, reads, writes, signal=None):
            if signal is None:
                signal = stop
            return cx.op('pe', lambda e: e.matmul(out, lhsT=lhsT, rhs=rhs, start=start, stop=stop),
                         reads=reads, writes=writes, signal=signal)

        def tp(out, in_, reads, writes, signal=True):
            return cx.op('pe', lambda e: e.transpose(out=out, in_=in_, identity=ident_f[0:in_.shape[0], 0:in_.shape[0]]),
                         reads=list(reads) + ['ident_f'], writes=writes, signal=signal)

        def cp(eng, out, in_, reads, writes, scale=None):
            if eng == 'act':
                if scale is None:
                    return cx.op('act', lambda e: e.activation(out=out, in_=in_, func=AF.Copy), reads, writes)
                return cx.op('act', lambda e: e.activation(out=out, in_=in_, func=AF.Identity, scale=scale), reads, writes)
            if scale is None:
                return cx.op(eng, lambda e: e.tensor_copy(out=out, in_=in_), reads, writes)
            return cx.op(eng, lambda e: e.tensor_scalar(out=out, in0=in_, scalar1=scale, scalar2=None, op0=ALU.mult),
                         reads, writes)

        def tt_(eng, out, in0, in1, op, reads, writes):
            return cx.op(eng, lambda e: e.tensor_tensor(out=out, in0=in0, in1=in1, op=op), reads, writes)

        def ts_(eng, out, in0, s1, s2, op0, op1, reads, writes):
            if s2 is None:
                return cx.op(eng, lambda e: e.tensor_scalar(out=out, in0=in0, scalar1=s1, scalar2=None, op0=op0),
                             reads, writes)
            return cx.op(eng, lambda e: e.tensor_scalar(out=out, in0=in0, scalar1=s1, scalar2=s2, op0=op0, op1=op1),
                         reads, writes)

        def stt(out, in0, scalar, in1, op0, op1, reads, writes):
            return cx.op('dve', lambda e: e.scalar_tensor_tensor(out=out, in0=in0, scalar=scalar, in1=in1,
                                                                 op0=op0, op1=op1), reads, writes)

        scnt = [0]

        def newstat():
            i = scnt[0]
            scnt[0] += 1
            assert i < 380
            return stat[:, i:i + 1], 'st%d' % i

        def sumsq(in_, nparts, reads):
            col, rn = newstat()
            n = in_.shape[-1]
            cx.op('act', lambda e: e.activation(out=sqs[0:nparts, 0:n], in_=in_, func=AF.Square,
                                                accum_out=col[0:nparts, :]),
                  reads=list(reads), writes=['sqs', rn])
            return col, rn

        def rstd_from(col, rn, nparts, n):
            cx.op('act', lambda e: e.activation(out=col[0:nparts, :], in_=col[0:nparts, :], func=AF.Sqrt,
                                                bias=eps_t[0:nparts, :], scale=1.0 / n), [rn, 'eps_t'], [rn])
            cx.op('dve', lambda e: e.reciprocal(out=col[0:nparts, :], in_=col[0:nparts, :]), [rn], [rn])

        def prologue():
            cx.dma('sp', gpre_c[:], gpre, writes=['gpre_c'])
            cx.dma('sp', gq_c[:], gq, writes=['gq_c'])
            cx.dma('sp', gkv_b[:], gkv, writes=['gkv_b'])
            cx.dma('sp', wconv_c[:], wconv, writes=['wconv_c'])
            cx.dma('sp', cos_t[:], r_cos, writes=['cos_t'])
            cx.dma('sp', sinm_t[:], r_sinm, writes=['sinm_t'])
            cx.dma('sp', cos_s[:], rs_cos, writes=['cos_s'])
            cx.dma('sp', sinm_s[:], rs_sinm, writes=['sinm_s'])
            cx.dma('sp', ct_s[:], rs_ct, writes=['ct_s'])
            cx.dma('sp', st_s[:], rs_st, writes=['st_s'])
            cx.dma('sp', sm_t[:], smask, writes=['sm_t'])
            cx.dma('sp', idx_t[:], ptab, writes=['idx_t'])
            cx.dma('sp', tmpD[:, 192:224], sconv.rearrange("p c b k -> p (c b k)"), writes=['tmpD'])
            cx.op('dve', lambda e: e.tensor_copy(out=uS[:, :, :, 0:2],
                                                 in_=tmpD[:, 192:224].rearrange("p (c b k) -> p c b k", c=4, b=4)),
                  reads=['tmpD'], writes=['uS'])
            cx.op('pool', lambda e: e.memset(ident_f[:], 0.0), writes=['ident_f'])
            cx.op('pool', lambda e: e.affine_select(out=ident_f[:], in_=ident_f[:], pattern=[[-1, 128]],
                                                    compare_op=ALU.not_equal, fill=1.0, base=0, channel_multiplier=1),
                  reads=['ident_f'], writes=['ident_f'])
            cx.op('pool', lambda e: e.tensor_copy(out=ident_b[:], in_=ident_f[:]), reads=['ident_f'], writes=['ident_b'])
            cx.op('pool', lambda e: e.memset(tmpA[:, 0:128], 0.0), writes=['tmpA'])
            cx.op('pool', lambda e: e.affine_select(out=tmpA[:, 0:128], in_=tmpA[:, 0:128], pattern=[[1, 128]],
                                                    compare_op=ALU.is_ge, fill=NEG, base=0, channel_multiplier=-1),
                  reads=['tmpA'], writes=['tmpA'])
            cx.op('pool', lambda e: e.tensor_copy(out=maskneg[:], in_=tmpA[:, 0:128]), reads=['tmpA'], writes=['maskneg'])
            cx.op('pool', lambda e: e.memset(ones_b[:], 1.0), writes=['ones_b'])
            cx.op('pool', lambda e: e.memset(eps_t[:], EPS), writes=['eps_t'])
            cx.op('pool', lambda e: e.memset(kst[0][:], 0.0), writes=['kst'])
            for d in range(4):
                cx.op('pool', lambda e, d=d: e.memset(uT[d][:, 0:2], 0.0), writes=['uT%d' % d])
            cx.op('pool', lambda e: e.memset(w_uqr_b[:], 0.0), writes=['w_uqr_b'])
            cx.op('pool', lambda e: e.memset(w_uq_b[:, :, 768:800], 0.0), writes=['w_uq_b'])
            cx.op('pool', lambda e: e.memset(w_uvp_b[:], 0.0), writes=['w_uvp_b'])
            for b in range(4):
                for g in range(16):
                    ts_('dve', idx_c[:, b, g:g + 1], idx_t[:, b:b + 1], 16, g, ALU.mult, ALU.add, ['idx_t'], ['idx_c'])
                for hf in range(2):
                    ts_('dve', idx_k[:, b, hf:hf + 1], idx_t[:, b:b + 1], 2, hf, ALU.mult, ALU.add, ['idx_t'], ['idx_k'])
            for (c0, c1, rn_) in ((0, 672, 'w_in_a'), (672, 2016, 'w_in_b'), (2016, DIN, 'w_in_b')):
                for kc in range(8):
                    cx.dma('pool', w_in_b[:, kc, c0:c1], w_in[kc * 128:(kc + 1) * 128, c0:c1],
                           writes=[rn_], key='d:' + rn_ + str(c0))
            stg = mixT[:, :, :].rearrange("p k n -> p (k n)").bitcast(F32)
            for kc in range(3):
                cx.dma('sp', stg[:, kc * 768:(kc + 1) * 768], w_uq[kc * 128:(kc + 1) * 128, :], writes=['stg_q%d' % kc])
            for kc in range(2):
                cx.dma('sp', stg[:, 2304 + kc * 1024:2304 + (kc + 1) * 1024], w_ukv[kc * 128:(kc + 1) * 128, :],
                       writes=['stg_k%d' % kc])
            for kc in range(3):
                sbuf = stg[:, kc * 768:(kc + 1) * 768]
                sres = 'stg_q%d' % kc
                ts_('dve', w_uq_b[:, kc, 0:768], sbuf, gq_c[:, kc:kc + 1], None, ALU.mult, None,
                    [sres, 'gq_c', 'w_uq_b'], ['w_uq_b'])
                src = sbuf.rearrange("p (h x) -> p h x", x=96)
                ts_('dve', w_uqr_b[:, kc, :, 64:80], src[:, :, 80:96], gq_c[:, kc:kc + 1], -1.0, ALU.mult, ALU.mult,
                    [sres, 'gq_c', 'w_uqr_b'], ['w_uqr_b'])
                ts_('dve', w_uqr_b[:, kc, :, 80:96], src[:, :, 64:80], gq_c[:, kc:kc + 1], None, ALU.mult, None,
                    [sres, 'gq_c', 'w_uqr_b'], ['w_uqr_b'])
            for kc in range(2):
                sbuf = stg[:, 2304 + kc * 1024:2304 + (kc + 1) * 1024]
                sres = 'stg_k%d' % kc
                cp('dve', w_ukv_b[:, kc, :], sbuf, [sres], ['w_ukv_b'])
                srcv = sbuf.rearrange("p (h x) -> p h x", x=128)
                hv = w_uvp_b[:, kc, :, :].rearrange("p (a two) c -> p a two c", two=2)
                sv = srcv.rearrange("p (a two) x -> p a two x", two=2)
                for par in range(2):
                    cp('dve', hv[:, :, par, par * 64:par * 64 + 64], sv[:, :, par, 64:128], [sres, 'w_uvp_b'], ['w_uvp_b'])
                for hq in range(2):
                    for hl in range(4):
                        h = hq * 4 + hl
                        tp(pb[5][0:64, hl * 128:(hl + 1) * 128], sbuf[:, h * 128:h * 128 + 64], [sres], ['pb5'],
                           signal=(hl == 3))
                    cp('dve', w_ukT_b[:, hq * 4:(hq + 1) * 4, kc * 128:(kc + 1) * 128],
                       pb[5][0:64, :].rearrange("p (a c) -> p a c", a=4), ['pb5'], ['w_ukT_b'])
            cx.op('dve', lambda e: e.memset(mixT[:, 0, 0:2], 0.0),
                  writes=['mixT', 'stg_q0', 'stg_q1', 'stg_q2', 'stg_k0', 'stg_k1'])
            yield

        def front_tile(NT, xt, xt_res, xn, xn_res, xT_dst, xT_res, pbx, pbx_res, stage='AB'):
            if 'A' in stage:
                col, rn = sumsq(xt[0:NT, :], NT, [xt_res])
                rstd_from(col, rn, NT, D)
                ts_('dve', xn[0:NT, :], xt[0:NT, :], col[0:NT, :], None, ALU.mult, None, [xt_res, rn], [xn_res])
            if 'B' in stage:
                for half in range(2):
                    bank = pbx[half]
                    for j in range(4):
                        kc = half * 4 + j
                        tp(bank[:, j * NT:(j + 1) * NT], xn[0:NT, kc * 128:(kc + 1) * 128], [xn_res], [pbx_res[half]],
                           signal=(j == 3))
                    for j in range(4):
                        kc = half * 4 + j
                        cp('dve' if j % 2 == 0 else 'act', xT_dst(kc), bank[:, j * NT:(j + 1) * NT],
                           [pbx_res[half], 'gpre_c'], [xT_res], scale=gpre_c[:, kc:kc + 1])

        def small_proj(NT, xT_src, xT_res, bA, bA_res, bB, bB_res, cos_ap, sinm_ap, tbl_res,
                       ckvn_t, ckvn_res, kst_t, kst_res, out_ckv, out_kpe,
                       cqT_dst, ckvT_dst, kpeT_dst, dst_res, pbt, pbt_res, kpe_col0, stage='ABC'):
            c0 = kpe_col0
            if 'A' in stage:
                for kc in range(8):
                    mm(bA[0:NT, 0:384], xT_src(kc), w_in_b[:, kc, 0:384], kc == 0, kc == 7, [xT_res, 'w_in_a'], [bA_res])
                for kc in range(8):
                    mm(bB[0:NT, 0:288], xT_src(kc), w_in_b[:, kc, 384:672], kc == 0, kc == 7, [xT_res, 'w_in_a'], [bB_res])
            if 'B' in stage:
                cq_col, cq_rn = sumsq(bA[0:NT, 0:384], NT, [bA_res])
                kv_col, kv_rn = sumsq(bB[0:NT, 0:256], NT, [bB_res])
                rstd_from(cq_col, cq_rn, NT, 384)
                rstd_from(kv_col, kv_rn, NT, 256)
                cp('act', cqn[0:NT, :], bA[0:NT, 0:384], [bA_res, cq_rn], ['cqn'], scale=cq_col[0:NT, :])
                stt(ckvn_t[0:NT, :], bB[0:NT, 0:256], kv_col[0:NT, :], gkv_b[0:NT, :], ALU.mult, ALU.mult,
                    [bB_res, kv_rn, 'gkv_b'], [ckvn_res])
                k = bB[0:NT, 256:288]
                tt_('dve', tmpD[0:NT, 0:32], k, cos_ap, ALU.mult, [bB_res] + list(tbl_res), ['tmpD'])
                tt_('dve', tmpD[0:NT, 32:48], bB[0:NT, 272:288], sinm_ap[:, 0:16], ALU.mult, [bB_res] + list(tbl_res), ['tmpD'])
                tt_('dve', tmpD[0:NT, 48:64], bB[0:NT, 256:272], sinm_ap[:, 16:32], ALU.mult, [bB_res] + list(tbl_res), ['tmpD'])
                tt_('dve', kst_t[0:NT, c0:c0 + 32], tmpD[0:NT, 0:32], tmpD[0:NT, 32:64], ALU.add, ['tmpD'], [kst_res])
                cx.dma('sp', out_ckv, ckvn_t[0:NT, :], reads=[ckvn_res], key='d:o_' + ckvn_res)
                cx.dma('sp', out_kpe, kst_t[0:NT, c0:c0 + 32], reads=[kst_res], key='d:o_' + kst_res)
            if 'C' in stage:
                for j in range(3):
                    tp(pbt[0][:, j * NT:(j + 1) * NT], cqn[0:NT, j * 128:(j + 1) * 128], ['cqn'], [pbt_res[0]], signal=(j == 2))
                for j in range(2):
                    tp(pbt[1][:, j * NT:(j + 1) * NT], ckvn_t[0:NT, j * 128:(j + 1) * 128], [ckvn_res], [pbt_res[1]], signal=False)
                tp(pbt[1][0:c0 + 32, 2 * NT:3 * NT], kst_t[0:NT, 0:c0 + 32], [kst_res], [pbt_res[1]])
                cp('act', cqT_dst, pbt[0][:, 0:3 * NT].rearrange("p (j t) -> p j t", j=3), [pbt_res[0]], [dst_res])
                cp('dve', ckvT_dst, pbt[1][:, 0:2 * NT].rearrange("p (j t) -> p j t", j=2), [pbt_res[1]], [dst_res])
                cp('dve', kpeT_dst, pbt[1][c0:c0 + 32, 2 * NT:3 * NT], [pbt_res[1]], [dst_res])

        def phase1():
            bankrot = [0]

            def load_x(tt):
                cx.dma('sp', xst[tt % 2][:], xp[tt * 128:(tt + 1) * 128, :], writes=['xst%d' % (tt % 2)])

            def fr(tt, stage):
                s_, t4 = tt % 2, tt % 4
                front_tile(128, xst[s_], 'xst%d' % s_, xst[s_], 'xst%d' % s_,
                           lambda kc: xT_b[:, kc, t4 * 128:(t4 + 1) * 128], 'xT_b',
                           [pb[0], pb[1]], ['pb0', 'pb1'], stage=stage)

            def sm(tt, stage):
                s_, t4 = tt % 2, tt % 4
                tok = slice(tt * 128, (tt + 1) * 128)
                small_proj(128, lambda kc: xT_b[:, kc, t4 * 128:(t4 + 1) * 128], 'xT_b',
                           pb[2], 'pb2', pb[3], 'pb3',
                           cos_t[:, tt, :], sinm_t[:, tt, :], ['cos_t', 'sinm_t'],
                           ckvn[s_], 'ckvn%d' % s_, kst[0], 'kst',
                           ckv_p[tok, :], kpe_p[tok, :],
                           cqT_b[:, :, tok], ckvT_b[:, :, tok], kpeT_b[64:96, tok], 'cT_b',
                           [pb[0], pb[1]], ['pb0', 'pb1'], 64, stage=stage)

            load_x(0)
            load_x(1)
            fr(0, 'A')
            for blk in range(4):
                for t4 in range(4):
                    tt = blk * 4 + t4
                    if tt + 1 < NTT:
                        fr(tt + 1, 'A')
                    fr(tt, 'B')
                    if tt + 2 < NTT:
                        load_x(tt + 2)
                    yield
                    sm(tt, 'A')
                    if t4 > 0:
                        sm(tt - 1, 'C')
                    sm(tt, 'B')
                    yield
                sm(blk * 4 + 3, 'C')
                yield
                bs = slice(blk * 512, (blk + 1) * 512)

                def bigmm(j):
                    bi = bankrot[0] % 5
                    bankrot[0] += 1
                    for kc in range(8):
                        mm(pb[bi][:, :], w_in_b[:, kc, 672 + j * 128:672 + (j + 1) * 128], xT_b[:, kc, :],
                           kc == 0, kc == 7, ['w_in_b', 'xT_b'], ['pb%d' % bi])
                    return pb[bi], 'pb%d' % bi

                for d in range(4):
                    bk, br = bigmm(d)
                    cx.op('act', lambda e, bk=bk, d=d: e.activation(out=mixT[:, d, bs], in_=bk[:, :], func=AF.Silu),
                          [br], ['mixT'])
                    yield
                for d in range(4):
                    bk, br = bigmm(8 + d)
                    cp('act', tmpA[:, :], bk[:, :], [br], ['tmpA'])
                    yield
                    bk, br = bigmm(12 + d)
                    tt_('dve', uT[d][:, 2:514], bk[:, :], tmpA[:, :], ALU.mult, [br, 'tmpA'], ['uT%d' % d])
                    ur = 'uT%d' % d
                    ts_('dve', tmpB[:, :], uT[d][:, 0:512], wconv_c[:, d, 0:1], None, ALU.mult, None,
                        [ur, 'wconv_c'], ['tmpB'])
                    stt(tmpB[:, :], uT[d][:, 1:513], wconv_c[:, d, 1:2], tmpB[:, :], ALU.mult, ALU.add,
                        [ur, 'tmpB'], ['tmpB'])
                    stt(tmpB[:, :], uT[d][:, 2:514], wconv_c[:, d, 2:3], tmpB[:, :], ALU.mult, ALU.add,
                        [ur, 'tmpB'], ['tmpB'])
                    if blk == 3:
                        cx.op('pool', lambda e, d=d: e.tensor_copy(out=cpst[:, d, :], in_=uT[d][:, 512:514]),
                              reads=[ur], writes=['cpst'])
                        if d == 3:
                            cx.dma('sp', conv_p.rearrange("p c k -> p (c k)"), cpst[:, :, :].rearrange("p c k -> p (c k)"),
                                   reads=['cpst'], key='d:o_convp')
                    else:
                        cx.op('pool', lambda e, d=d: e.tensor_copy(out=uT[d][:, 0:2], in_=uT[d][:, 512:514]),
                              [ur], [ur])
                    yield
                    bk, br = bigmm(16 + d)
                    cx.op('act', lambda e, bk=bk: e.activation(out=tmpC[:, :], in_=bk[:, :], func=AF.Silu),
                          [br], ['tmpC'])
                    yield
                    bk, br = bigmm(4 + d)
                    tt_('dve', tmpB[:, :], bk[:, :], tmpB[:, :], ALU.mult, [br, 'tmpB'], ['tmpB'])
                    tt_('pool', mixT[:, 4 + d, bs], tmpB[:, :], tmpC[:, :], ALU.mult, ['tmpB', 'tmpC'], ['mixT'])
                    yield

        def phase2():
            phase_now[0] = 2
            cx.op('pool', lambda e: e.memset(Vh[:, :, 64:65], 1.0), writes=['w_in_a', 'w_in_b', 'late', 'Vh'])
            cx.op('dve', lambda e: e.memset(KT[64:128, :], 0.0), reads=['late'], writes=['KT'])
            cx.op('dve', lambda e: e.memset(QT[64:128, :], 0.0), reads=['late'], writes=['QT'])
            wout_jobs = [(kc, hf) for kc in range(8) for hf in range(2)]
            wout_done = [0]

            def wout_some(k):
                for _ in range(k):
                    if wout_jobs:
                        kc, hf = wout_jobs.pop(0)
                        cx.dma('pool', w_out_b[:, kc, hf * 512:(hf + 1) * 512],
                               w_out[kc * 128:(kc + 1) * 128, hf * 512:(hf + 1) * 512],
                               reads=['late'], writes=['w_out_b'] if wout_done[0] == 0 else [],
                               key='d:w_out')
                        wout_done[0] += 1
                if not wout_jobs and 'w_out_b' in cx.res and not wout_state.get('final'):
                    cx.res['w_out_b'] = {'w': ('d:w_out', cx.dcnt['d:w_out']), 'r': {}}
                    wout_state['final'] = True
            cx.dma('sp', gpost_b, gpost, reads=['late'], writes=['gpost_b'])
            srot = [0]
            orot = [0]
            cx.op('dve', lambda e: e.memset(KT1[64:128, :], 0.0), reads=['late'], writes=['xst0'])
            cx.op('dve', lambda e: e.memset(QT1[64:128, :], 0.0), reads=['late'], writes=['xT_b'])
            cx.op('pool', lambda e: e.memset(Vh1[:, :, 64:65], 1.0), reads=['late', 'xT_b'], writes=['Vh1'])
            cx.res['QT1'] = cx.res['xT_b']
            QTs, KTs, Vhs = [QT, QT1], [KT, KT1], [Vh, Vh1]
            QTr, KTr, Vhr = ['QT', 'QT1'], ['KT', 'xst0'], ['Vh', 'Vh1']

            def produce(h):
                z = h % 2
                KTz, QTz, Vhz = KTs[z], QTs[z], Vhs[z]
                cx.op('pool', lambda e: e.tensor_copy(out=KTz[64:96, :], in_=kpeT_b[64:96, :]),
                      ['cT_b', 'late'], [KTr[z]])
                wout_some(4)
                for blk in range(4):
                    bs = slice(blk * 512, (blk + 1) * 512)
                    for kc in range(2):
                        mm(pb[4][:, :], w_ukv_b[:, kc, h * 128:h * 128 + 128], ckvT_b[:, kc, bs], kc == 0, kc == 1,
                           ['w_ukv_b', 'cT_b'], ['pb4'])
                    cp('dve', KTz[0:64, bs], pb[4][0:64, :], ['pb4'], [KTr[z]])
                    yield
                for half in range(2):
                    for k8 in range(8):
                        kt = half * 8 + k8
                        for kc in range(2):
                            mm(pb[4][:, k8 * 64:(k8 + 1) * 64], ckvT_b[:, kc, kt * 128:(kt + 1) * 128],
                               w_ukv_b[:, kc, h * 128 + 64:h * 128 + 128], kc == 0, kc == 1,
                               ['w_ukv_b', 'cT_b'], ['pb4'], signal=(k8 == 7 and kc == 1))
                    cp('act', Vhz[:, half * 8:(half + 1) * 8, 0:64], pb[4][:, :].rearrange("p (k v) -> p k v", v=64),
                       ['pb4'], [Vhr[z]])
                    yield
                for blk in range(4):
                    bs = slice(blk * 512, (blk + 1) * 512)
                    cx.dma('sp', ct_blk[:], r_ct[:, bs], reads=['late'], writes=['ct_blk'])
                    cx.dma('sp', st_blk[:], r_st[:, bs], reads=['late'], writes=['st_blk'])
                    for kc in range(3):
                        mm(pb[4][:, :], w_uq_b[:, kc, h * 96:h * 96 + 128], cqT_b[:, kc, bs], kc == 0, kc == 2,
                           ['w_uq_b', 'cT_b'], ['pb4'])
                    tt_('dve', tmpA[0:96, :], pb[4][0:96, :], ct_blk[:], ALU.mult, ['pb4', 'ct_blk'], ['tmpA'])
                    yield
                    for kc in range(3):
                        mm(pb[4][0:96, :], w_uqr_b[:, kc, h, :], cqT_b[:, kc, bs], kc == 0, kc == 2,
                           ['w_uqr_b', 'cT_b'], ['pb4'])
                    tt_('dve', tmpB[0:96, :], pb[4][0:96, :], st_blk[:], ALU.mult, ['pb4', 'st_blk'], ['tmpB'])
                    tt_('pool', QTz[0:96, bs], tmpA[0:96, :], tmpB[0:96, :], ALU.add, ['tmpA', 'tmpB'], [QTr[z]])
                    yield

            def attention(h):
                z = h % 2
                KTz, QTz, Vhz = KTs[z], QTs[z], Vhs[z]
                items = [(qb, kt) for qb in range(4) for kt in range((qb + 1) * 4)]
                obank = {}
                for qb in range(4):
                    obank[qb] = 2 + (orot[0] % 2)
                    orot[0] += 1
                base = srot[0]
                srot[0] += len(items)

                def emit_S(i):
                    qb, kt = items[i]
                    qlo = max(kt, qb * 4)
                    ncol = ((qb + 1) * 4 - qlo) * 128
                    sb = (base + i) % 2
                    pi = (base + i) % 3
                    sbr = 'pb%d' % sb
                    diag = kt >= qb * 4
                    mm(pb[sb][:, 0:ncol], KTz[:, kt * 128:(kt + 1) * 128], QTz[:, qlo * 128:(qb + 1) * 512],
                       True, not diag, [KTr[z], QTr[z]], [sbr])
                    if diag:
                        mm(pb[sb][:, 0:128], ident_b[:], maskneg[:], False, True, ['ident_b', 'maskneg'], [sbr])
                    cx.op('act', lambda e: e.activation(
                        out=pT[pi][:, 0:ncol], in_=pb[sb][:, 0:ncol], func=AF.Exp, scale=SCALE),
                        [sbr], ['pT%d' % pi])

                def emit_PV(i):
                    qb, kt = items[i]
                    qlo = max(kt, qb * 4)
                    pi = (base + i) % 3
                    ob = obank[qb]
                    obr = 'pb%d' % ob
                    if kt == 0:
                        mm(pb[ob][:, 0:260], ident_b[:], zeros260, True, False, ['ident_b', 'w_uqr_b'], [obr],
                           signal=False)
                    for qt in range(qlo, (qb + 1) * 4):
                        j = qt - qb * 4
                        c = (qt - qlo) * 128
                        last = (kt == (qb + 1) * 4 - 1)
                        mm(pb[ob][:, j * 65:(j + 1) * 65], pT[pi][:, c:c + 128], Vhz[:, kt, :],
                           False, last, ['pT%d' % pi, Vhr[z]], [obr],
                           signal=(qt == (qb + 1) * 4 - 1))
                    if kt == (qb + 1) * 4 - 1:
                        oview = pb[ob][:, 0:260].rearrange("p (j v) -> p j v", v=65)
                        col, rn = newstat()
                        rec = stat[:, scnt[0]:scnt[0] + 4]
                        scnt[0] += 4
                        cx.op('dve', lambda e: e.reciprocal(out=rec, in_=oview[:, :, 64]), [obr], [rn])
                        for j in range(4):
                            qt = qb * 4 + j
                            cp('dve' if j % 2 == 0 else 'act', opair[:, qt, (h % 2) * 64:(h % 2) * 64 + 64],
                               oview[:, j, 0:64], [obr, rn], ['opair'], scale=rec[:, j:j + 1])

                emit_S(0)
                for i in range(len(items)):
                    if i + 1 < len(items):
                        emit_S(i + 1)
                    emit_PV(i)
                    yield

            def otrans(h):
                hp = h // 2
                for blk in range(4):
                    bs = slice(blk * 512, (blk + 1) * 512)
                    for j in range(4):
                        tp(pb[4][:, j * 128:(j + 1) * 128], opair[:, blk * 4 + j, :], ['opair'], ['pb4'],
                           signal=(j == 3))
                    tt_('dve', mixT[:, hp, bs], pb[4][:, :], mixT[:, hp, bs], ALU.mult, ['pb4', 'mixT'], ['mixT'])
                    yield

            for _ in produce(0):
                yield
            for h in range(8):
                ga = attention(h)
                gp = produce(h + 1) if h + 1 < 8 else None
                k = 0
                for _ in ga:
                    yield
                    k += 1
                    if gp is not None and k % 2 == 0:
                        try:
                            next(gp)
                            yield
                        except StopIteration:
                            gp = None
                if gp is not None:
                    for _ in gp:
                        yield
                if h % 2 == 1:
                    for _ in otrans(h):
                        yield

        def out_tile(NT, mix_src, mix_res, xres, xres_res, out_ap, bA, bA_res, bB, bB_res, okey,
                     ysb_t=None, ysb_res='ysb'):
            if ysb_t is None:
                ysb_t = ysb
            for half, (bk, br) in enumerate(((bA, bA_res), (bB, bB_res))):
                for kc in range(8):
                    mm(bk[0:NT, :], mix_src(kc), w_out_b[:, kc, half * 512:(half + 1) * 512], kc == 0, kc == 7,
                       [mix_res, 'w_out_b'], [br])
            c1, r1 = sumsq(bA[0:NT, :], NT, [bA_res])
            c2, r2 = sumsq(bB[0:NT, :], NT, [bB_res])
            tt_('dve', c1[0:NT, :], c1[0:NT, :], c2[0:NT, :], ALU.add, [r1, r2], [r1])
            rstd_from(c1, r1, NT, D)
            for half, (bk, br) in enumerate(((bA, bA_res), (bB, bB_res))):
                hs = slice(half * 512, (half + 1) * 512)
                tt_('dve', ysb_t[0:NT, hs], bk[0:NT, :], gpost_b[0:NT, hs], ALU.mult, [br, 'gpost_b'], [ysb_res])
                stt(ysb_t[0:NT, hs], ysb_t[0:NT, hs], c1[0:NT, :], xres[0:NT, hs], ALU.mult, ALU.add,
                    [ysb_res, r1, xres_res], [ysb_res])
            cx.dma('sp', out_ap, ysb_t[0:NT, :], reads=[ysb_res], key=okey)

        def phase3():
            phase_now[0] = 3
            npair = 4 if sample_done[0] else 2
            ybufs = [(ysb, 'ysb'), (QT.bitcast(F32), 'QT'), (KT.bitcast(F32), 'KT')]
            cx.dma('sp', xst[0][:], xp[0:128, :], writes=['xst0'])
            for tt in range(NTT):
                s = tt % 2
                if tt + 1 < NTT:
                    cx.dma('sp', xst[1 - s][:], xp[(tt + 1) * 128:(tt + 2) * 128, :], writes=['xst%d' % (1 - s)])
                bp = tt % npair
                yb, yr = ybufs[tt % 3]
                out_tile(128, lambda kc, tt=tt: mixT[:, kc, tt * 128:(tt + 1) * 128], 'mixT',
                         xst[s], 'xst%d' % s, y_p[tt * 128:(tt + 1) * 128, :],
                         pb[2 * bp], 'pb%d' % (2 * bp), pb[2 * bp + 1], 'pb%d' % (2 * bp + 1), 'd:o_' + yr,
                         ysb_t=yb, ysb_res=yr)
                yield

        def sample():
            P5, P6, P7 = pb[5], pb[6], pb[7]
            xs_t = xst[1]
            cx.dma('sp', xs_t[0:16, :], xs, writes=['xst1'])
            front_tile(16, xs_t, 'xst1', xs_t, 'xst1', lambda kc: xTs[:, kc, :], 'xTs', [P5, P6], ['pb5', 'pb6'])
            small_proj(16, lambda kc: xTs[:, kc, :], 'xTs', P5, 'pb5', P6, 'pb6',
                       cos_s[:, :], sinm_s[:, :], ['cos_s', 'sinm_s'],
                       ckvn[0], 'ckvn0', kst[0], 'kst', ckv_s[:, :], kpe_s[:, :],
                       cqTs[:, :, :], ckvTs[:, :, :], kpeTs[:, :], 'cTs', [P5, P6], ['pb5', 'pb6'], 0)
            cp('pool', ckvn_sb[:, :], ckvn[0][0:16, :], ['ckvn0'], ['ckvn_sb'])
            cx.op('pool', lambda e: e.memset(kst[0][:, 0:64], 0.0), ['kst'], ['kst'])
            yield
            for h in range(8):
                for kc in range(3):
                    mm(P5[0:64, 0:16], w_uq_b[:, kc, h * 96:h * 96 + 64], cqTs[:, kc, :], kc == 0, kc == 2,
                       ['w_uq_b', 'cTs'], ['pb5'])
                for kc in range(3):
                    mm(P5[0:32, 16:32], w_uq_b[:, kc, h * 96 + 64:h * 96 + 96], cqTs[:, kc, :], kc == 0, kc == 2,
                       ['w_uq_b', 'cTs'], ['pb5'])
                for kc in range(3):
                    mm(P5[0:32, 32:48], w_uqr_b[:, kc, h, 64:96], cqTs[:, kc, :], kc == 0, kc == 2,
                       ['w_uqr_b', 'cTs'], ['pb5'])
                cp('act', qnT_s[:, :], P5[0:64, 0:16], ['pb5'], ['qnT_s'])
                tt_('dve', tmpD[0:32, 112:128], P5[0:32, 16:32], ct_s[:, :], ALU.mult, ['pb5', 'ct_s'], ['tmpD'])
                tt_('dve', tmpD[0:32, 128:144], P5[0:32, 32:48], st_s[:, :], ALU.mult, ['pb5', 'st_s'], ['tmpD'])
                tt_('dve', qpeT[:, :, h, :], tmpD[0:32, 112:128].rearrange("p (b i) -> p b i", b=4),
                    tmpD[0:32, 128:144].rearrange("p (b i) -> p b i", b=4), ALU.add, ['tmpD'], ['qpeT'])
                for kc in range(2):
                    mm(P6[:, kc * 16:(kc + 1) * 16], w_ukT_b[:, h, kc * 128:(kc + 1) * 128], qnT_s[:, :], True, True,
                       ['w_ukT_b', 'qnT_s'], ['pb6'])
                cp('dve', qlatT[:, :, :, h, :], P6[:, 0:32].rearrange("p (k b i) -> p k b i", k=2, b=4),
                   ['pb6'], ['qlatT'])
                yield
            groups = [(b, g) for b in range(4) for g in range(16)]
            NG = len(groups)

            def stage_A(n):
                b, g = groups[n]
                s3 = n % 3
                cx.dma('pool', gck[s3][:, :], cckv, reads=['idx_c'], writes=['gck%d' % s3],
                       indirect=bass.IndirectOffsetOnAxis(ap=idx_c[:, b, g:g + 1], axis=0))
                if g % 8 == 0:
                    cx.dma('pool', gkp[:, :], ckpe, reads=['idx_k'], writes=['gkp'],
                           indirect=bass.IndirectOffsetOnAxis(ap=idx_k[:, b, g // 8:g // 8 + 1], axis=0))

            def stage_B(n):
                b, g = groups[n]
                s3 = n % 3
                s = n % 2
                for r in range(4):
                    for tl in range(2):
                        t = r * 2 + tl
                        for kc in range(2):
                            mm(P5[:, (tl * 2 + kc) * 128:(tl * 2 + kc + 1) * 128],
                               gck[s3][:, t * 256 + kc * 128:t * 256 + (kc + 1) * 128], ident_b[:], True, True,
                               ['gck%d' % s3, 'ident_b'], ['pb5'], signal=(tl == 1 and kc == 1))
                    cp('dve' if r % 2 == 0 else 'act',
                       ckTs[s][:, :, r * 2:r * 2 + 2, :].rearrange("p k t n -> p t k n"),
                       P5[:, :].rearrange("p (t k n) -> p t k n", t=2, k=2), ['pb5'], ['ckTs%d' % s])
                    for tl in range(2):
                        t = r * 2 + tl
                        tk = (g % 8) * 8 + t
                        mm(P6[0:32, 256 + tl * 128:256 + (tl + 1) * 128], gkp[:, tk * 32:(tk + 1) * 32], ident_b[:],
                           True, True, ['gkp', 'ident_b'], ['pb6'], signal=(tl == 1))
                    cp('act' if r % 2 == 0 else 'dve', kpTs[s][:, r * 2:(r + 1) * 2, :],
                       P6[0:32, 256:512].rearrange("p (t n) -> p t n", t=2), ['pb6'], ['kpTs%d' % s])
                    yield

            def stage_C1(n):
                b, g = groups[n]
                s = n % 2
                for t in range(8):
                    o = pb[6][:, t * 32:(t + 1) * 32]
                    for kc in range(2):
                        mm(o, ckTs[s][:, kc, t, :], qlatT[:, kc, b, :, :].rearrange("p h i -> p (h i)"),
                           kc == 0, False, ['ckTs%d' % s, 'qlatT'], ['pb6'], signal=False)
                    mm(o, kpTs[s][:, t, :], qpeT[:, b, :, :].rearrange("p h i -> p (h i)"), False, True,
                       ['kpTs%d' % s, 'qpeT'], ['pb6'], signal=(t == 7))
                cx.op('act', lambda e, s=s: e.activation(out=pTs[s][:, :], in_=pb[6][:, 0:256], func=AF.Exp,
                                                         scale=SCALE), ['pb6'], ['pTs%d' % s])

            def stage_C2(n):
                b, g = groups[n]
                s = n % 2
                s3 = n % 3
                if g == 0:
                    mm(pb[7][0:32, 0:260], ident_b[:, 0:32], zeros260, True, False, ['ident_b', 'w_uqr_b'], ['pb7'],
                       signal=False)
                for t in range(8):
                    mm(pb[7][0:32, 0:256], pTs[s][:, t * 32:(t + 1) * 32], gck[s3][:, t * 256:(t + 1) * 256],
                       False, False, ['pTs%d' % s, 'gck%d' % s3], ['pb7'], signal=False)
                for t in range(8):
                    mm(pb[7][0:32, 256:257], pTs[s][:, t * 32:(t + 1) * 32], ones_b[:, 0:1],
                       False, False, ['pTs%d' % s, 'ones_b'], ['pb7'], signal=(t == 7))

            def batch_tail(b):
                o = pb[6][0:16, 0:32]
                for kc in range(2):
                    mm(o, ckvTs[:, kc, :], qlatT[:, kc, b, :, :].rearrange("p h i -> p (h i)"), kc == 0, False,
                       ['cTs', 'qlatT'], ['pb6'], signal=False)
                mm(o, kpeTs[:, :], qpeT[:, b, :, :].rearrange("p h i -> p (h i)"), False, True,
                   ['cTs', 'qpeT'], ['pb6'])
                cx.op('act', lambda e: e.activation(out=tmpD[0:16, 144:176], in_=pb[6][0:16, 0:32], func=AF.Exp,
                                                    scale=SCALE), ['pb6'], ['tmpD'])
                tt_('dve', pTn[:, :], tmpD[0:16, 144:176], sm_t[:, b, :], ALU.mult, ['tmpD', 'sm_t'], ['pTn'])
                mm(pb[7][0:32, 0:256], pTn[:, :], ckvn_sb[:, :], False, False, ['pTn', 'ckvn_sb'], ['pb7'], signal=False)
                mm(pb[7][0:32, 256:257], pTn[:, :], ones_b[0:16, 0:1], False, True, ['pTn', 'ones_b'], ['pb7'])
                cp('dve', ssm[:, 0:1], pb[7][0:32, 256:257], ['pb7'], ['ssm'])
                cx.op('dve', lambda e: e.reciprocal(out=ssm[:, 1:2], in_=ssm[:, 0:1]), ['ssm'], ['ssm'])
                cp('dve', olat[:, :], pb[7][0:32, 0:256], ['pb7', 'ssm'], ['olat'], scale=ssm[:, 1:2])
                for kc in range(2):
                    tp(P5[:, kc * 32:(kc + 1) * 32], olat[:, kc * 128:(kc + 1) * 128], ['olat'], ['pb5'], signal=(kc == 1))
                cp('dve', olatT[:, :, :, b * 4:(b + 1) * 4], P5[:, 0:64].rearrange("p (k h i) -> p k h i", k=2, h=8),
                   ['pb5'], ['olatT'])

            def feature_major():
                def smm(j, bank, col):
                    for kc in range(8):
                        mm(bank[:, col:col + 16], w_in_b[:, kc, 672 + j * 128:672 + (j + 1) * 128], xTs[:, kc, :],
                           kc == 0, kc == 7, ['w_in_b', 'xTs'], ['pb5'])
                for d in range(4):
                    smm(d, P5, 0)
                    cx.op('act', lambda e, d=d: e.activation(out=mixTs[:, d, :], in_=P5[:, 0:16], func=AF.Silu),
                          ['pb5'], ['mixTs'])
                yield
                for d in range(4):
                    smm(8 + d, P5, 0)
                    smm(12 + d, P5, 16)
                    smm(16 + d, P5, 32)
                    smm(4 + d, P5, 48)
                    cp('act', tmpD[:, 64:80], P5[:, 0:16], ['pb5'], ['tmpD'])
                    uv = uS[:, d, :, :]
                    tt_('dve', uv[:, :, 2:6], P5[:, 16:32].rearrange("p (b i) -> p b i", b=4),
                        tmpD[:, 64:80].rearrange("p (b i) -> p b i", b=4), ALU.mult, ['pb5', 'tmpD'], ['uS'])
                    yv = tmpD[:, 80:96].rearrange("p (b i) -> p b i", b=4)
                    ts_('dve', yv, uv[:, :, 0:4], wconv_c[:, d, 0:1], None, ALU.mult, None, ['uS', 'wconv_c'], ['tmpD'])
                    stt(yv, uv[:, :, 1:5], wconv_c[:, d, 1:2], yv, ALU.mult, ALU.add, ['uS', 'tmpD'], ['tmpD'])
                    stt(yv, uv[:, :, 2:6], wconv_c[:, d, 2:3], yv, ALU.mult, ALU.add, ['uS', 'tmpD'], ['tmpD'])
                    cx.op('act', lambda e: e.activation(out=tmpD[:, 96:112], in_=P5[:, 32:48], func=AF.Silu),
                          ['pb5'], ['tmpD'])
                    tt_('dve', tmpD[:, 80:96], P5[:, 48:64], tmpD[:, 80:96], ALU.mult, ['pb5', 'tmpD'], ['tmpD'])
                    tt_('dve', mixTs[:, 4 + d, :], tmpD[:, 80:96], tmpD[:, 96:112], ALU.mult, ['tmpD'], ['mixTs'])
                    yield
                cx.op('dve', lambda e: e.tensor_copy(out=tmpD[:, 224:256].rearrange("p (c b k) -> p c b k", c=4, b=4),
                                                     in_=uS[:, :, :, 4:6]), reads=['uS'], writes=['tmpD'])
                cx.dma('sp', conv_s.rearrange("p c b k -> p (c b k)"), tmpD[:, 224:256], reads=['tmpD'], key='d:o_convs')

            stage_A(0)
            stage_A(1)
            for _ in stage_B(0):
                yield
            for _ in feature_major():
                yield
            for n in range(NG):
                stage_C1(n)
                yield
                if n + 1 < NG:
                    for _ in stage_B(n + 1):
                        yield
                if n + 2 < NG:
                    stage_A(n + 2)
                stage_C2(n)
                if groups[n][1] == 15:
                    batch_tail(groups[n][0])
                yield
            for hp in range(4):
                n = 0
                for hh in range(2):
                    h = hp * 2 + hh
                    for kc in range(2):
                        mm(P5[:, 0:16], w_uvp_b[:, kc, h, :], olatT[:, kc, h, :], n == 0, n == 3,
                           ['w_uvp_b', 'olatT'], ['pb5'])
                        n += 1
                tt_('dve', mixTs[:, hp, :], P5[:, 0:16], mixTs[:, hp, :], ALU.mult, ['pb5', 'mixTs'], ['mixTs'])
            yield
            assert wout_state.get('final'), 'w_out not fully issued before the sample epilogue'
            cx.dma('sp', xs_t[0:16, :], xs, writes=['xst1'])
            out_tile(16, lambda kc: mixTs[:, kc, :], 'mixTs', xs_t, 'xst1', y_s[:, :], P5, 'pb5', P6, 'pb6', 'd:o_ysb')
            sample_done[0] = True
            yield

        for _ in prologue():
            pass

        def chain(*gens):
            for g_ in gens:
                for _ in g_:
                    yield

        P = chain(phase1(), phase2(), phase3())
        S = sample()
        next(S)
        p_alive, s_alive = True, True
        acc = 0.0
        RATIOS = {1: 0.8, 2: 0.9, 3: 1.0}
        while p_alive or s_alive:
            if p_alive:
                try:
                    next(P)
                except StopIteration:
                    p_alive = False
            acc += RATIOS[phase_now[0]]
            while s_alive and (acc >= 1.0 or not p_alive):
                acc -= 1.0
                try:
                    next(S)
                except StopIteration:
                    s_alive = False
        cx.finish('sp')
        build_nc.stats = dict(nwaits=cx.nwaits, cnt=dict(cx.cnt), nsem=len(cx.dsem) + 5, late=off[0])
    return nc


_NC = None


def _rope_tables():
    inv = (np.float32(10000.0) ** (-(np.arange(0, 32, 2, dtype=np.float32)) / np.float32(32))).astype(np.float32)

    def cs(pos):
        ang = (pos.astype(np.float32)[:, None] * inv[None, :]).astype(np.float32)
        ang = np.concatenate([ang, ang], axis=-1).astype(np.float64)
        return np.cos(ang).astype(np.float32), np.sin(ang).astype(np.float32)

    cp_, sp_ = cs(np.arange(SEQ))
    sgn = np.concatenate([-np.ones(16, np.float32), np.ones(16, np.float32)])
    r_cos = np.ascontiguousarray(cp_.reshape(NTT, 128, 32).transpose(1, 0, 2))
    r_sinm = np.ascontiguousarray((sp_ * sgn).reshape(NTT, 128, 32).transpose(1, 0, 2))
    r_ct = np.ones((96, SEQ), np.float32)
    r_st = np.zeros((96, SEQ), np.float32)
    r_ct[64:96] = cp_.T
    r_st[64:96] = sp_.T
    pos_s = PAST + (np.arange(16) % 4)
    cs_, ss_ = cs(pos_s)
    smask = np.zeros((16, 4, 32), np.float32)
    for b in range(4):
        for k in range(4):
            for h in range(8):
                for i in range(4):
                    if k <= i:
                        smask[b * 4 + k, b, h * 4 + i] = 1.0
    return dict(r_cos=r_cos, r_sinm=r_sinm, r_ct=r_ct, r_st=r_st,
                rs_cos=cs_, rs_sinm=(ss_ * sgn).astype(np.float32),
                rs_ct=np.ascontiguousarray(cs_.T), rs_st=np.ascontiguousarray(ss_.T), smask=smask)


def kernel(x_prompt, x_sample, cache_ckv, cache_kpe, state_conv, page_table,
           g_pre, w_in, g_qnorm, w_uq, g_kvnorm, w_ukv, w_conv, w_out, g_post):
    global _NC
    if _NC is None:
        _NC = build_nc()
    nc = _NC
    f = lambda a: np.ascontiguousarray(np.asarray(a, dtype=np.float32))
    x_prompt, x_sample = f(x_prompt), f(x_sample)
    tabs = _rope_tables()
    cckv = f(cache_ckv).reshape(NPHYS * 16, 2048)
    ckpe = f(cache_kpe).reshape(NPHYS * 2, 2048)
    pt = np.asarray(page_table, dtype=np.int32)
    sc = f(state_conv)[0]
    shared = dict(
        cckv=cckv, ckpe=ckpe,
        gpre=np.ascontiguousarray(f(g_pre)[0].reshape(8, 128).T),
        w_in=f(w_in)[0],
        gq=np.ascontiguousarray(f(g_qnorm)[0].reshape(3, 128).T),
        w_uq=f(w_uq)[0],
        gkv=np.ascontiguousarray(np.broadcast_to(f(g_kvnorm)[0][None, :], (128, 256))),
        w_ukv=f(w_ukv)[0],
        wconv=np.ascontiguousarray(f(w_conv)[0].reshape(3, 4, 128).transpose(2, 1, 0)),
        w_out=f(w_out)[0],
        gpost=np.ascontiguousarray(np.broadcast_to(f(g_post)[0][None, :], (128, D))),
        **tabs)
    in_maps = []
    for c in range(8):
        m = dict(shared)
        m["xp"] = x_prompt[c]
        m["xs"] = np.ascontiguousarray(x_sample[4 * c:4 * c + 4].reshape(16, D))
        m["ptab"] = np.ascontiguousarray(pt[4 * c:4 * c + 4].T)
        scc = sc[4 * c:4 * c + 4]
        m["sconv"] = np.ascontiguousarray(scc.reshape(4, 2, 4, 128).transpose(3, 2, 0, 1))
        in_maps.append(m)
    res = run_bass_kernel_spmd(nc, in_maps, core_ids=list(range(8)))
    R = res.results
    g = lambda k, c: np.asarray(R[c][k], dtype=np.float32)
    y_prompt = np.stack([g("y_p", c) for c in range(8)])
    y_sample = np.concatenate([g("y_s", c).reshape(4, 4, D) for c in range(8)])
    ckv_p = np.stack([g("ckv_p", c) for c in range(8)])[None]
    kpe_p = np.stack([g("kpe_p", c) for c in range(8)])[None]
    conv_p = np.stack([g("conv_p", c).transpose(2, 1, 0).reshape(2, 512) for c in range(8)])[None]
    ckv_s = np.concatenate([g("ckv_s", c).reshape(4, 4, 256) for c in range(8)])[None]
    kpe_s = np.concatenate([g("kpe_s", c).reshape(4, 4, 32) for c in range(8)])[None]
    conv_s = np.concatenate([g("conv_s", c).transpose(2, 3, 1, 0).reshape(4, 2, 512) for c in range(8)])[None]
    return (y_prompt, y_sample, ckv_p, kpe_p, conv_p, ckv_s, kpe_s, conv_s)
```

```python
import numpy as np
from contextlib import ExitStack
import concourse.bass as bass
import concourse.mybir as mybir
from concourse.bass_utils import run_bass_kernel_spmd

F32 = mybir.dt.float32
BF16 = mybir.dt.bfloat16
I32 = mybir.dt.int32
AF = mybir.ActivationFunctionType
ALU = mybir.AluOpType

D = 1024
SEQ = 2048
NTT = 16
DIN = 3232
PAST = 16384
NPHYS = 5120
EPS = 1e-6
SCALE = 96 ** -0.5
NEG = -30000.0


class Ctx:
    def __init__(self, nc, stack, same_engine_sync=True):
        self.nc = nc
        self.stack = stack
        self.E = {'pe': nc.tensor, 'act': nc.scalar, 'dve': nc.vector, 'pool': nc.gpsimd, 'sp': nc.sync}
        self.sem = {}
        self.cnt = {}
        for e in self.E:
            self.sem[e] = stack.enter_context(nc.semaphore("s_" + e))
            self.cnt[e] = 0
        self.seen = {e: {} for e in self.E}
        self.res = {}
        self.dsem = {}
        self.dcnt = {}
        self.same = same_engine_sync
        self.nwaits = 0

    def _r(self, name):
        if name not in self.res:
            self.res[name] = {'w': None, 'r': {}}
        return self.res[name]

    def _deps(self, reads, writes):
        deps = []
        for r in reads:
            w = self._r(r)['w']
            if w is not None:
                deps.append(w)
        for wn in writes:
            rr = self._r(wn)
            if rr['w'] is not None:
                deps.append(rr['w'])
            deps.extend(rr['r'].items())
        return deps

    def _wait(self, e, deps):
        eng = self.E[e]
        best = {}
        for (key, val) in deps:
            if key == e and (e == 'pe' or not self.same):
                continue
            if val > best.get(key, 0):
                best[key] = val
        for key, val in best.items():
            if self.seen[e].get(key, 0) >= val:
                continue
            s = self.sem[key] if key in self.sem else self.dsem[key]
            eng.wait_ge(s, val)
            self.nwaits += 1
            self.seen[e][key] = val

    def _record(self, tok, reads, writes):
        for r in reads:
            d = self._r(r)['r']
            if tok[1] > d.get(tok[0], 0):
                d[tok[0]] = tok[1]
        for wn in writes:
            self.res[wn] = {'w': tok, 'r': {}}

    def op(self, e, fn, reads=(), writes=(), signal=True):
        self._wait(e, self._deps(reads, writes))
        inst = fn(self.E[e])
        if signal:
            self.cnt[e] += 1
            inst.then_inc(self.sem[e], 1)
            tok = (e, self.cnt[e])
        else:
            tok = (e, self.cnt[e] + 1)
        self._record(tok, reads, writes)
        return inst

    def dma(self, q, out, in_, reads=(), writes=(), key=None, indirect=None):
        if key is None:
            key = 'd:' + (writes[0] if writes else reads[0])
        if key not in self.dsem:
            self.dsem[key] = self.stack.enter_context(self.nc.semaphore("s_" + key.replace(':', '_')))
            self.dcnt[key] = 0
        self._wait(q, self._deps(reads, writes))
        if indirect is None:
            inst = self.E[q].dma_start(out=out, in_=in_)
        else:
            inst = self.E[q].indirect_dma_start(out=out, out_offset=None, in_=in_, in_offset=indirect)
        self.dcnt[key] += 16
        inst.then_inc(self.dsem[key], 16)
        self._record((key, self.dcnt[key]), reads, writes)
        return inst

    def finish(self, e='sp'):
        deps = [(k, v) for k, v in self.dcnt.items()]
        for k in self.E:
            if k != e and self.cnt[k] > 0:
                deps.append((k, self.cnt[k]))
        self._wait(e, deps)


def build_nc():
    nc = bass.Bass("TRN2", target_bir_lowering=False)

    def din(name, shape, dt=F32):
        return nc.dram_tensor(name, list(shape), dt, kind="ExternalInput").ap()

    def dout(name, shape, dt=F32):
        return nc.dram_tensor(name, list(shape), dt, kind="ExternalOutput").ap()

    xp = din("xp", [SEQ, D])
    xs = din("xs", [16, D])
    cckv = din("cckv", [NPHYS * 16, 2048])
    ckpe = din("ckpe", [NPHYS * 2, 2048])
    sconv = din("sconv", [128, 4, 4, 2])
    ptab = din("ptab", [128, 4], I32)
    gpre = din("gpre", [128, 8])
    w_in = din("w_in", [D, DIN])
    gq = din("gq", [128, 3])
    w_uq = din("w_uq", [384, 768])
    gkv = din("gkv", [128, 256])
    w_ukv = din("w_ukv", [256, 1024])
    wconv = din("wconv", [128, 4, 3])
    w_out = din("w_out", [D, D])
    gpost = din("gpost", [128, D])
    r_cos = din("r_cos", [128, NTT, 32])
    r_sinm = din("r_sinm", [128, NTT, 32])
    r_ct = din("r_ct", [96, SEQ])
    r_st = din("r_st", [96, SEQ])
    rs_cos = din("rs_cos", [16, 32])
    rs_sinm = din("rs_sinm", [16, 32])
    rs_ct = din("rs_ct", [32, 16])
    rs_st = din("rs_st", [32, 16])
    smask = din("smask", [16, 4, 32])

    y_p = dout("y_p", [SEQ, D])
    y_s = dout("y_s", [16, D])
    ckv_p = dout("ckv_p", [SEQ, 256])
    kpe_p = dout("kpe_p", [SEQ, 32])
    conv_p = dout("conv_p", [128, 4, 2])
    ckv_s = dout("ckv_s", [16, 256])
    kpe_s = dout("kpe_s", [16, 32])
    conv_s = dout("conv_s", [128, 4, 4, 2])

    with ExitStack() as st:
        cx = Ctx(nc, st)

        def T(name, shape, dt):
            return st.enter_context(nc.sbuf_tensor(name, list(shape), dt))

        pb = [st.enter_context(nc.psum_tensor("pb%d" % i, [128, 512], F32)) for i in range(8)]

        wbig = T("wbig", [128, 8 * DIN], BF16)
        w_in_b = wbig[:, :].rearrange("p (k n) -> p k n", k=8)
        ident_f = T("ident_f", [128, 128], F32)
        ident_b = T("ident_b", [128, 128], BF16)
        maskneg = T("maskneg", [128, 128], BF16)
        ones_b = T("ones_b", [128, 2], BF16)
        eps_t = T("eps_t", [128, 1], F32)
        gpre_c = T("gpre_c", [128, 8], F32)
        gq_c = T("gq_c", [128, 3], F32)
        gkv_b = T("gkv_b", [128, 256], F32)
        wconv_c = T("wconv_c", [128, 4, 3], F32)
        w_uq_b = T("w_uq_b", [128, 3, 800], BF16)
        w_uqr_b = T("w_uqr_b", [128, 3, 8, 96], BF16)
        w_ukv_b = T("w_ukv_b", [128, 2, 1024], BF16)
        w_ukT_b = T("w_ukT_b", [64, 8, 256], BF16)
        w_uvp_b = T("w_uvp_b", [128, 2, 8, 128], BF16)
        cos_t = T("cos_t", [128, NTT, 32], F32)
        sinm_t = T("sinm_t", [128, NTT, 32], F32)
        xst = [T("xst%d" % i, [128, D], F32) for i in range(2)]
        sqs = T("sqs", [128, D], BF16)
        stat = T("stat", [128, 384], F32)
        xT_b = T("xT_b", [128, 8, 512], BF16)
        cqT_b = T("cqT_b", [128, 3, SEQ], BF16)
        ckvT_b = T("ckvT_b", [128, 2, SEQ], BF16)
        kpeT_b = T("kpeT_b", [96, SEQ], BF16)
        mixT = T("mixT", [128, 8, SEQ], BF16)
        uT = [T("uT%d" % d, [128, 514], F32) for d in range(4)]
        tmpA = T("tmpA", [128, 512], F32)
        tmpB = T("tmpB", [128, 512], F32)
        tmpC = T("tmpC", [128, 512], F32)
        tmpD = T("tmpD", [128, 512], F32)
        cqn = T("cqn", [128, 384], F32)
        ckvn = [T("ckvn%d" % i, [128, 256], F32) for i in range(2)]
        kst1 = T("kst", [128, 96], F32)
        kst = [kst1, kst1]
        xTs = T("xTs", [128, 8, 16], BF16)
        cqTs = T("cqTs", [128, 3, 16], BF16)
        ckvTs = T("ckvTs", [128, 2, 16], BF16)
        kpeTs = T("kpeTs", [32, 16], BF16)
        ckvn_sb = T("ckvn_sb", [16, 256], BF16)
        mixTs = T("mixTs", [128, 8, 16], BF16)
        uS = T("uS", [128, 4, 4, 6], F32)
        sm_t = T("sm_t", [16, 4, 32], F32)
        cos_s = T("cos_s", [16, 32], F32)
        sinm_s = T("sinm_s", [16, 32], F32)
        ct_s = T("ct_s", [32, 16], F32)
        st_s = T("st_s", [32, 16], F32)
        qlatT = T("qlatT", [128, 2, 4, 8, 4], BF16)
        qpeT = T("qpeT", [32, 4, 8, 4], BF16)
        qnT_s = T("qnT_s", [64, 16], BF16)
        olatT = T("olatT", [128, 2, 8, 16], BF16)
        idx_t = T("idx_t", [128, 4], I32)
        idx_c = T("idx_c", [128, 4, 16], I32)
        idx_k = T("idx_k", [128, 4, 2], I32)
        gck = [T("gck%d" % i, [128, 2048], BF16) for i in range(3)]
        gkp = T("gkp", [128, 2048], BF16)
        ckTs = [T("ckTs%d" % i, [128, 2, 8, 128], BF16) for i in range(2)]
        kpTs = [T("kpTs%d" % i, [32, 8, 128], BF16) for i in range(2)]
        pTs = [T("pTs%d" % i, [128, 256], BF16) for i in range(2)]
        pTn = T("pTn", [16, 32], BF16)
        ssm = T("ssm", [32, 8], F32)
        cpst = T("cpst", [128, 4, 2], F32)

        LA = {}
        off = [0]

        def late(name, parts, nelem_bf16):
            a = off[0]
            off[0] += nelem_bf16
            assert off[0] <= 8 * DIN
            LA[name] = wbig[0:parts, a:a + nelem_bf16]
            return LA[name]

        w_out_b = late("w_out", 128, 8 * D).rearrange("p (k n) -> p k n", k=8)
        QT = late("QT", 128, SEQ)
        KT = late("KT", 128, SEQ)
        Vh = late("Vh", 128, 16 * 65 + 16)[:, 0:16 * 65].rearrange("p (k v) -> p k v", v=65)
        pT = [late("pT%d" % i, 128, 512) for i in range(3)]
        opair = late("opair", 128, 2 * 16 * 128).bitcast(F32).rearrange("p (q v) -> p q v", v=128)
        gpost_b = late("gpost", 128, 2 * D).bitcast(F32)
        ysb = late("ysb", 128, 2 * 1024).bitcast(F32)
        ct_blk = late("ct_blk", 96, 1024).bitcast(F32)
        st_blk = late("st_blk", 96, 1024).bitcast(F32)
        olat = tmpD[0:32, 256:512]
        LATE = 'wbig'
        xTflat = xT_b[:, :, :].rearrange("p k n -> p (k n)")
        QT1 = xTflat[:, 0:SEQ]
        Vh1 = xTflat[:, SEQ:SEQ + 16 * 65].rearrange("p (k v) -> p k v", v=65)
        KT1 = xst[0][:, :].bitcast(BF16)
        wout_state = {}
        sample_done = [False]
        phase_now = [1]

        zeros260 = w_uqr_b[:, 0, 0:5, 0:52]
        def mm(out, lhsT, rhs, start, stop, reads, writes, signal=None):
            if signal is None:
                signal = stop
            return cx.op('pe', lambda e: e.matmul(out, lhsT=lhsT, rhs=rhs, start=start, stop=stop),
                         reads=reads, writes=writes, signal=signal)

        def tp(out, in_, reads, writes, signal=True):
            return cx.op('pe', lambda e: e.transpose(out=out, in_=in_, identity=ident_f[0:in_.shape[0], 0:in_.shape[0]]),
                         reads=list(reads) + ['ident_f'], writes=writes, signal=signal)

        def cp(eng, out, in_, reads, writes, scale=None):
            if eng == 'act':
                if scale is None:
                    return cx.op('act', lambda e: e.activation(out=out, in_=in_, func=AF.Copy), reads, writes)
                return cx.op('act', lambda e: e.activation(out=out, in_=in_, func=AF.Identity, scale=scale), reads, writes)
            if scale is None:
                return cx.op(eng, lambda e: e.tensor_copy(out=out, in_=in_), reads, writes)
            return cx.op(eng, lambda e: e.tensor_scalar(out=out, in0=in_, scalar1=scale, scalar2=None, op0=ALU.mult),
                         reads, writes)

        def tt_(eng, out, in0, in1, op, reads, writes):
            return cx.op(eng, lambda e: e.tensor_tensor(out=out, in0=in0, in1=in1, op=op), reads, writes)

        def ts_(eng, out, in0, s1, s2, op0, op1, reads, writes):
            if s2 is None:
                return cx.op(eng, lambda e: e.tensor_scalar(out=out, in0=in0, scalar1=s1, scalar2=None, op0=op0),
                             reads, writes)
            return cx.op(eng, lambda e: e.tensor_scalar(out=out, in0=in0, scalar1=s1, scalar2=s2, op0=op0, op1=op1),
                         reads, writes)

        def stt(out, in0, scalar, in1, op0, op1, reads, writes):
            return cx.op('dve', lambda e: e.scalar_tensor_tensor(out=out, in0=in0, scalar=scalar, in1=in1,
                                                                 op0=op0, op1=op1), reads, writes)

        scnt = [0]

        def newstat():
            i = scnt[0]
            scnt[0] += 1
            assert i < 380
            return stat[:, i:i + 1], 'st%d' % i

        def sumsq(in_, nparts, reads):
            col, rn = newstat()
            n = in_.shape[-1]
            cx.op('act', lambda e: e.activation(out=sqs[0:nparts, 0:n], in_=in_, func=AF.Square,
                                                accum_out=col[0:nparts, :]),
                  reads=list(reads), writes=['sqs', rn])
            return col, rn

        def rstd_from(col, rn, nparts, n):
            cx.op('act', lambda e: e.activation(out=col[0:nparts, :], in_=col[0:nparts, :], func=AF.Sqrt,
                                                bias=eps_t[0:nparts, :], scale=1.0 / n), [rn, 'eps_t'], [rn])
            cx.op('dve', lambda e: e.reciprocal(out=col[0:nparts, :], in_=col[0:nparts, :]), [rn], [rn])

        def prologue():
            cx.dma('sp', gpre_c[:], gpre, writes=['gpre_c'])
            cx.dma('sp', gq_c[:], gq, writes=['gq_c'])
            cx.dma('sp', gkv_b[:], gkv, writes=['gkv_b'])
            cx.dma('sp', wconv_c[:], wconv, writes=['wconv_c'])
            cx.dma('sp', cos_t[:], r_cos, writes=['cos_t'])
            cx.dma('sp', sinm_t[:], r_sinm, writes=['sinm_t'])
            cx.dma('sp', cos_s[:], rs_cos, writes=['cos_s'])
            cx.dma('sp', sinm_s[:], rs_sinm, writes=['sinm_s'])
            cx.dma('sp', ct_s[:], rs_ct, writes=['ct_s'])
            cx.dma('sp', st_s[:], rs_st, writes=['st_s'])
            cx.dma('sp', sm_t[:], smask, writes=['sm_t'])
            cx.dma('sp', idx_t[:], ptab, writes=['idx_t'])
            cx.dma('sp', tmpD[:, 192:224], sconv.rearrange("p c b k -> p (c b k)"), writes=['tmpD'])
            cx.op('dve', lambda e: e.tensor_copy(out=uS[:, :, :, 0:2],
                                                 in_=tmpD[:, 192:224].rearrange("p (c b k) -> p c b k", c=4, b=4)),
                  reads=['tmpD'], writes=['uS'])
            cx.op('pool', lambda e: e.memset(ident_f[:], 0.0), writes=['ident_f'])
            cx.op('pool', lambda e: e.affine_select(out=ident_f[:], in_=ident_f[:], pattern=[[-1, 128]],
                                                    compare_op=ALU.not_equal, fill=1.0, base=0, channel_multiplier=1),
                  reads=['ident_f'], writes=['ident_f'])
            cx.op('pool', lambda e: e.tensor_copy(out=ident_b[:], in_=ident_f[:]), reads=['ident_f'], writes=['ident_b'])
            cx.op('pool', lambda e: e.memset(tmpA[:, 0:128], 0.0), writes=['tmpA'])
            cx.op('pool', lambda e: e.affine_select(out=tmpA[:, 0:128], in_=tmpA[:, 0:128], pattern=[[1, 128]],
                                                    compare_op=ALU.is_ge, fill=NEG, base=0, channel_multiplier=-1),
                  reads=['tmpA'], writes=['tmpA'])
            cx.op('pool', lambda e: e.tensor_copy(out=maskneg[:], in_=tmpA[:, 0:128]), reads=['tmpA'], writes=['maskneg'])
            cx.op('pool', lambda e: e.memset(ones_b[:], 1.0), writes=['ones_b'])
            cx.op('pool', lambda e: e.memset(eps_t[:], EPS), writes=['eps_t'])
            cx.op('pool', lambda e: e.memset(kst[0][:], 0.0), writes=['kst'])
            for d in range(4):
                cx.op('pool', lambda e, d=d: e.memset(uT[d][:, 0:2], 0.0), writes=['uT%d' % d])
            cx.op('pool', lambda e: e.memset(w_uqr_b[:], 0.0), writes=['w_uqr_b'])
            cx.op('pool', lambda e: e.memset(w_uq_b[:, :, 768:800], 0.0), writes=['w_uq_b'])
            cx.op('pool', lambda e: e.memset(w_uvp_b[:], 0.0), writes=['w_uvp_b'])
            for b in range(4):
                for g in range(16):
                    ts_('dve', idx_c[:, b, g:g + 1], idx_t[:, b:b + 1], 16, g, ALU.mult, ALU.add, ['idx_t'], ['idx_c'])
                for hf in range(2):
                    ts_('dve', idx_k[:, b, hf:hf + 1], idx_t[:, b:b + 1], 2, hf, ALU.mult, ALU.add, ['idx_t'], ['idx_k'])
            for (c0, c1, rn_) in ((0, 672, 'w_in_a'), (672, 2016, 'w_in_b'), (2016, DIN, 'w_in_b')):
                for kc in range(8):
                    cx.dma('pool', w_in_b[:, kc, c0:c1], w_in[kc * 128:(kc + 1) * 128, c0:c1],
                           writes=[rn_], key='d:' + rn_ + str(c0))
            stg = mixT[:, :, :].rearrange("p k n -> p (k n)").bitcast(F32)
            for kc in range(3):
                cx.dma('sp', stg[:, kc * 768:(kc + 1) * 768], w_uq[kc * 128:(kc + 1) * 128, :], writes=['stg_q%d' % kc])
            for kc in range(2):
                cx.dma('sp', stg[:, 2304 + kc * 1024:2304 + (kc + 1) * 1024], w_ukv[kc * 128:(kc + 1) * 128, :],
                       writes=['stg_k%d' % kc])
            for kc in range(3):
                sbuf = stg[:, kc * 768:(kc + 1) * 768]
                sres = 'stg_q%d' % kc
                ts_('dve', w_uq_b[:, kc, 0:768], sbuf, gq_c[:, kc:kc + 1], None, ALU.mult, None,
                    [sres, 'gq_c', 'w_uq_b'], ['w_uq_b'])
                src = sbuf.rearrange("p (h x) -> p h x", x=96)
                ts_('dve', w_uqr_b[:, kc, :, 64:80], src[:, :, 80:96], gq_c[:, kc:kc + 1], -1.0, ALU.mult, ALU.mult,
                    [sres, 'gq_c', 'w_uqr_b'], ['w_uqr_b'])
                ts_('dve', w_uqr_b[:, kc, :, 80:96], src[:, :, 64:80], gq_c[:, kc:kc + 1], None, ALU.mult, None,
                    [sres, 'gq_c', 'w_uqr_b'], ['w_uqr_b'])
            for kc in range(2):
                sbuf = stg[:, 2304 + kc * 1024:2304 + (kc + 1) * 1024]
                sres = 'stg_k%d' % kc
                cp('dve', w_ukv_b[:, kc, :], sbuf, [sres], ['w_ukv_b'])
                srcv = sbuf.rearrange("p (h x) -> p h x", x=128)
                hv = w_uvp_b[:, kc, :, :].rearrange("p (a two) c -> p a two c", two=2)
                sv = srcv.rearrange("p (a two) x -> p a two x", two=2)
                for par in range(2):
                    cp('dve', hv[:, :, par, par * 64:par * 64 + 64], sv[:, :, par, 64:128], [sres, 'w_uvp_b'], ['w_uvp_b'])
                for hq in range(2):
                    for hl in range(4):
                        h = hq * 4 + hl
                        tp(pb[5][0:64, hl * 128:(hl + 1) * 128], sbuf[:, h * 128:h * 128 + 64], [sres], ['pb5'],
                           signal=(hl == 3))
                    cp('dve', w_ukT_b[:, hq * 4:(hq + 1) * 4, kc * 128:(kc + 1) * 128],
                       pb[5][0:64, :].rearrange("p (a c) -> p a c", a=4), ['pb5'], ['w_ukT_b'])
            cx.op('dve', lambda e: e.memset(mixT[:, 0, 0:2], 0.0),
                  writes=['mixT', 'stg_q0', 'stg_q1', 'stg_q2', 'stg_k0', 'stg_k1'])
            yield

        def front_tile(NT, xt, xt_res, xn, xn_res, xT_dst, xT_res, pbx, pbx_res, stage='AB'):
            if 'A' in stage:
                col, rn = sumsq(xt[0:NT, :], NT, [xt_res])
                rstd_from(col, rn, NT, D)
                cp('act', xn[0:NT, :], xt[0:NT, :], [xt_res, rn], [xn_res], scale=col[0:NT, :])
            if 'B' in stage:
                for half in range(2):
                    bank = pbx[half]
                    for j in range(4):
                        kc = half * 4 + j
                        tp(bank[:, j * NT:(j + 1) * NT], xn[0:NT, kc * 128:(kc + 1) * 128], [xn_res], [pbx_res[half]],
                           signal=(j == 3))
                    for j in range(4):
                        kc = half * 4 + j
                        cp('dve' if j % 2 == 0 else 'act', xT_dst(kc), bank[:, j * NT:(j + 1) * NT],
                           [pbx_res[half], 'gpre_c'], [xT_res], scale=gpre_c[:, kc:kc + 1])

        def small_proj(NT, xT_src, xT_res, bA, bA_res, bB, bB_res, cos_ap, sinm_ap, tbl_res,
                       ckvn_t, ckvn_res, kst_t, kst_res, out_ckv, out_kpe,
                       cqT_dst, ckvT_dst, kpeT_dst, dst_res, pbt, pbt_res, kpe_col0, stage='ABC'):
            c0 = kpe_col0
            if 'A' in stage:
                for kc in range(8):
                    mm(bA[0:NT, 0:384], xT_src(kc), w_in_b[:, kc, 0:384], kc == 0, kc == 7, [xT_res, 'w_in_a'], [bA_res])
                for kc in range(8):
                    mm(bB[0:NT, 0:288], xT_src(kc), w_in_b[:, kc, 384:672], kc == 0, kc == 7, [xT_res, 'w_in_a'], [bB_res])
            if 'B' in stage:
                cq_col, cq_rn = sumsq(bA[0:NT, 0:384], NT, [bA_res])
                kv_col, kv_rn = sumsq(bB[0:NT, 0:256], NT, [bB_res])
                rstd_from(cq_col, cq_rn, NT, 384)
                rstd_from(kv_col, kv_rn, NT, 256)
                cp('act', cqn[0:NT, :], bA[0:NT, 0:384], [bA_res, cq_rn], ['cqn'], scale=cq_col[0:NT, :])
                stt(ckvn_t[0:NT, :], bB[0:NT, 0:256], kv_col[0:NT, :], gkv_b[0:NT, :], ALU.mult, ALU.mult,
                    [bB_res, kv_rn, 'gkv_b'], [ckvn_res])
                k = bB[0:NT, 256:288]
                tt_('dve', tmpD[0:NT, 0:32], k, cos_ap, ALU.mult, [bB_res] + list(tbl_res), ['tmpD'])
                tt_('dve', tmpD[0:NT, 32:48], bB[0:NT, 272:288], sinm_ap[:, 0:16], ALU.mult, [bB_res] + list(tbl_res), ['tmpD'])
                tt_('dve', tmpD[0:NT, 48:64], bB[0:NT, 256:272], sinm_ap[:, 16:32], ALU.mult, [bB_res] + list(tbl_res), ['tmpD'])
                tt_('dve', kst_t[0:NT, c0:c0 + 32], tmpD[0:NT, 0:32], tmpD[0:NT, 32:64], ALU.add, ['tmpD'], [kst_res])
                cx.dma('sp', out_ckv, ckvn_t[0:NT, :], reads=[ckvn_res], key='d:o_' + ckvn_res)
                cx.dma('sp', out_kpe, kst_t[0:NT, c0:c0 + 32], reads=[kst_res], key='d:o_' + kst_res)
            if 'C' in stage:
                for j in range(3):
                    tp(pbt[0][:, j * NT:(j + 1) * NT], cqn[0:NT, j * 128:(j + 1) * 128], ['cqn'], [pbt_res[0]], signal=(j == 2))
                for j in range(2):
                    tp(pbt[1][:, j * NT:(j + 1) * NT], ckvn_t[0:NT, j * 128:(j + 1) * 128], [ckvn_res], [pbt_res[1]], signal=False)
                tp(pbt[1][0:c0 + 32, 2 * NT:3 * NT], kst_t[0:NT, 0:c0 + 32], [kst_res], [pbt_res[1]])
                cp('act', cqT_dst, pbt[0][:, 0:3 * NT].rearrange("p (j t) -> p j t", j=3), [pbt_res[0]], [dst_res])
                cp('dve', ckvT_dst, pbt[1][:, 0:2 * NT].rearrange("p (j t) -> p j t", j=2), [pbt_res[1]], [dst_res])
                cp('dve', kpeT_dst, pbt[1][c0:c0 + 32, 2 * NT:3 * NT], [pbt_res[1]], [dst_res])

        def phase1():
            bankrot = [0]

            def load_x(tt):
                cx.dma('sp', xst[tt % 2][:], xp[tt * 128:(tt + 1) * 128, :], writes=['xst%d' % (tt % 2)])

            def fr(tt, stage):
                s_, t4 = tt % 2, tt % 4
                front_tile(128, xst[s_], 'xst%d' % s_, xst[s_], 'xst%d' % s_,
                           lambda kc: xT_b[:, kc, t4 * 128:(t4 + 1) * 128], 'xT_b',
                           [pb[0], pb[1]], ['pb0', 'pb1'], stage=stage)

            def sm(tt, stage):
                s_, t4 = tt % 2, tt % 4
                tok = slice(tt * 128, (tt + 1) * 128)
                small_proj(128, lambda kc: xT_b[:, kc, t4 * 128:(t4 + 1) * 128], 'xT_b',
                           pb[2], 'pb2', pb[3], 'pb3',
                           cos_t[:, tt, :], sinm_t[:, tt, :], ['cos_t', 'sinm_t'],
                           ckvn[s_], 'ckvn%d' % s_, kst[0], 'kst',
                           ckv_p[tok, :], kpe_p[tok, :],
                           cqT_b[:, :, tok], ckvT_b[:, :, tok], kpeT_b[64:96, tok], 'cT_b',
                           [pb[0], pb[1]], ['pb0', 'pb1'], 64, stage=stage)

            load_x(0)
            load_x(1)
            fr(0, 'A')
            for blk in range(4):
                for t4 in range(4):
                    tt = blk * 4 + t4
                    if tt + 1 < NTT:
                        fr(tt + 1, 'A')
                    fr(tt, 'B')
                    if tt + 2 < NTT:
                        load_x(tt + 2)
                    yield
                    sm(tt, 'A')
                    if t4 > 0:
                        sm(tt - 1, 'C')
                    sm(tt, 'B')
                    yield
                sm(blk * 4 + 3, 'C')
                yield
                bs = slice(blk * 512, (blk + 1) * 512)

                def bigmm(j):
                    bi = bankrot[0] % 5
                    bankrot[0] += 1
                    for kc in range(8):
                        mm(pb[bi][:, :], w_in_b[:, kc, 672 + j * 128:672 + (j + 1) * 128], xT_b[:, kc, :],
                           kc == 0, kc == 7, ['w_in_b', 'xT_b'], ['pb%d' % bi])
                    return pb[bi], 'pb%d' % bi

                for d in range(4):
                    bk, br = bigmm(d)
                    cx.op('act', lambda e, bk=bk, d=d: e.activation(out=mixT[:, d, bs], in_=bk[:, :], func=AF.Silu),
                          [br], ['mixT'])
                    yield
                for d in range(4):
                    bk, br = bigmm(8 + d)
                    cp('act', tmpA[:, :], bk[:, :], [br], ['tmpA'])
                    yield
                    bk, br = bigmm(12 + d)
                    tt_('dve', uT[d][:, 2:514], bk[:, :], tmpA[:, :], ALU.mult, [br, 'tmpA'], ['uT%d' % d])
                    ur = 'uT%d' % d
                    cp('act', tmpB[:, :], uT[d][:, 0:512], [ur, 'wconv_c'], ['tmpB'], scale=wconv_c[:, d, 0:1])
                    stt(tmpB[:, :], uT[d][:, 1:513], wconv_c[:, d, 1:2], tmpB[:, :], ALU.mult, ALU.add,
                        [ur, 'tmpB'], ['tmpB'])
                    stt(tmpB[:, :], uT[d][:, 2:514], wconv_c[:, d, 2:3], tmpB[:, :], ALU.mult, ALU.add,
                        [ur, 'tmpB'], ['tmpB'])
                    if blk == 3:
                        cx.op('pool', lambda e, d=d: e.tensor_copy(out=cpst[:, d, :], in_=uT[d][:, 512:514]),
                              reads=[ur], writes=['cpst'])
                        if d == 3:
                            cx.dma('sp', conv_p.rearrange("p c k -> p (c k)"), cpst[:, :, :].rearrange("p c k -> p (c k)"),
                                   reads=['cpst'], key='d:o_convp')
                    else:
                        cx.op('pool', lambda e, d=d: e.tensor_copy(out=uT[d][:, 0:2], in_=uT[d][:, 512:514]),
                              [ur], [ur])
                    yield
                    bk, br = bigmm(16 + d)
                    cx.op('act', lambda e, bk=bk: e.activation(out=tmpC[:, :], in_=bk[:, :], func=AF.Silu),
                          [br], ['tmpC'])
                    yield
                    bk, br = bigmm(4 + d)
                    tt_('dve', tmpB[:, :], bk[:, :], tmpB[:, :], ALU.mult, [br, 'tmpB'], ['tmpB'])
                    tt_('pool', mixT[:, 4 + d, bs], tmpB[:, :], tmpC[:, :], ALU.mult, ['tmpB', 'tmpC'], ['mixT'])
                    yield

        def phase2():
            phase_now[0] = 2
            cx.op('pool', lambda e: e.memset(Vh[:, :, 64:65], 1.0), writes=['w_in_a', 'w_in_b', 'late', 'Vh'])
            cx.op('dve', lambda e: e.memset(KT[64:128, :], 0.0), reads=['late'], writes=['KT'])
            cx.op('dve', lambda e: e.memset(QT[64:128, :], 0.0), reads=['late'], writes=['QT'])
            wout_jobs = [(kc, hf) for kc in range(8) for hf in range(2)]
            wout_done = [0]

            def wout_some(k):
                for _ in range(k):
                    if wout_jobs:
                        kc, hf = wout_jobs.pop(0)
                        cx.dma('pool', w_out_b[:, kc, hf * 512:(hf + 1) * 512],
                               w_out[kc * 128:(kc + 1) * 128, hf * 512:(hf + 1) * 512],
                               reads=['late'], writes=['w_out_b'] if wout_done[0] == 0 else [],
                               key='d:w_out')
                        wout_done[0] += 1
                if not wout_jobs and 'w_out_b' in cx.res and not wout_state.get('final'):
                    cx.res['w_out_b'] = {'w': ('d:w_out', cx.dcnt['d:w_out']), 'r': {}}
                    wout_state['final'] = True
            cx.dma('sp', gpost_b, gpost, reads=['late'], writes=['gpost_b'])
            srot = [0]
            orot = [0]
            cx.op('dve', lambda e: e.memset(KT1[64:128, :], 0.0), reads=['late'], writes=['xst0'])
            cx.op('dve', lambda e: e.memset(QT1[64:128, :], 0.0), reads=['late'], writes=['xT_b'])
            cx.op('pool', lambda e: e.memset(Vh1[:, :, 64:65], 1.0), reads=['late', 'xT_b'], writes=['Vh1'])
            cx.res['QT1'] = cx.res['xT_b']
            QTs, KTs, Vhs = [QT, QT1], [KT, KT1], [Vh, Vh1]
            QTr, KTr, Vhr = ['QT', 'QT1'], ['KT', 'xst0'], ['Vh', 'Vh1']

            def produce(h):
                z = h % 2
                KTz, QTz, Vhz = KTs[z], QTs[z], Vhs[z]
                cx.op('pool', lambda e: e.tensor_copy(out=KTz[64:96, :], in_=kpeT_b[64:96, :]),
                      ['cT_b', 'late'], [KTr[z]])
                wout_some(4)
                for blk in range(4):
                    bs = slice(blk * 512, (blk + 1) * 512)
                    for kc in range(2):
                        mm(pb[4][:, :], w_ukv_b[:, kc, h * 128:h * 128 + 128], ckvT_b[:, kc, bs], kc == 0, kc == 1,
                           ['w_ukv_b', 'cT_b'], ['pb4'])
                    cp('dve', KTz[0:64, bs], pb[4][0:64, :], ['pb4'], [KTr[z]])
                    yield
                for half in range(2):
                    for k8 in range(8):
                        kt = half * 8 + k8
                        for kc in range(2):
                            mm(pb[4][:, k8 * 64:(k8 + 1) * 64], ckvT_b[:, kc, kt * 128:(kt + 1) * 128],
                               w_ukv_b[:, kc, h * 128 + 64:h * 128 + 128], kc == 0, kc == 1,
                               ['w_ukv_b', 'cT_b'], ['pb4'], signal=(k8 == 7 and kc == 1))
                    cp('act', Vhz[:, half * 8:(half + 1) * 8, 0:64], pb[4][:, :].rearrange("p (k v) -> p k v", v=64),
                       ['pb4'], [Vhr[z]])
                    yield
                for blk in range(4):
                    bs = slice(blk * 512, (blk + 1) * 512)
                    cx.dma('sp', ct_blk[:], r_ct[:, bs], reads=['late'], writes=['ct_blk'])
                    cx.dma('sp', st_blk[:], r_st[:, bs], reads=['late'], writes=['st_blk'])
                    for kc in range(3):
                        mm(pb[4][:, :], w_uq_b[:, kc, h * 96:h * 96 + 128], cqT_b[:, kc, bs], kc == 0, kc == 2,
                           ['w_uq_b', 'cT_b'], ['pb4'])
                    tt_('dve', tmpA[0:96, :], pb[4][0:96, :], ct_blk[:], ALU.mult, ['pb4', 'ct_blk'], ['tmpA'])
                    yield
                    for kc in range(3):
                        mm(pb[4][0:96, :], w_uqr_b[:, kc, h, :], cqT_b[:, kc, bs], kc == 0, kc == 2,
                           ['w_uqr_b', 'cT_b'], ['pb4'])
                    tt_('dve', tmpB[0:96, :], pb[4][0:96, :], st_blk[:], ALU.mult, ['pb4', 'st_blk'], ['tmpB'])
                    tt_('pool', QTz[0:96, bs], tmpA[0:96, :], tmpB[0:96, :], ALU.add, ['tmpA', 'tmpB'], [QTr[z]])
                    yield

            def attention(h):
                z = h % 2
                KTz, QTz, Vhz = KTs[z], QTs[z], Vhs[z]
                items = [(qb, kt) for qb in range(4) for kt in range((qb + 1) * 4)]
                obank = {}
                for qb in range(4):
                    obank[qb] = 2 + (orot[0] % 2)
                    orot[0] += 1
                base = srot[0]
                srot[0] += len(items)

                def emit_S(i):
                    qb, kt = items[i]
                    qlo = max(kt, qb * 4)
                    ncol = ((qb + 1) * 4 - qlo) * 128
                    sb = (base + i) % 2
                    pi = (base + i) % 3
                    sbr = 'pb%d' % sb
                    diag = kt >= qb * 4
                    mm(pb[sb][:, 0:ncol], KTz[:, kt * 128:(kt + 1) * 128], QTz[:, qlo * 128:(qb + 1) * 512],
                       True, not diag, [KTr[z], QTr[z]], [sbr])
                    if diag:
                        mm(pb[sb][:, 0:128], ident_b[:], maskneg[:], False, True, ['ident_b', 'maskneg'], [sbr])
                    cx.op('act', lambda e: e.activation(
                        out=pT[pi][:, 0:ncol], in_=pb[sb][:, 0:ncol], func=AF.Exp, scale=SCALE),
                        [sbr], ['pT%d' % pi])

                def emit_PV(i):
                    qb, kt = items[i]
                    qlo = max(kt, qb * 4)
                    pi = (base + i) % 3
                    ob = obank[qb]
                    obr = 'pb%d' % ob
                    if kt == 0:
                        mm(pb[ob][:, 0:260], ident_b[:], zeros260, True, False, ['ident_b', 'w_uqr_b'], [obr],
                           signal=False)
                    for qt in range(qlo, (qb + 1) * 4):
                        j = qt - qb * 4
                        c = (qt - qlo) * 128
                        last = (kt == (qb + 1) * 4 - 1)
                        mm(pb[ob][:, j * 65:(j + 1) * 65], pT[pi][:, c:c + 128], Vhz[:, kt, :],
                           False, last, ['pT%d' % pi, Vhr[z]], [obr],
                           signal=(qt == (qb + 1) * 4 - 1))
                    if kt == (qb + 1) * 4 - 1:
                        oview = pb[ob][:, 0:260].rearrange("p (j v) -> p j v", v=65)
                        col, rn = newstat()
                        rec = stat[:, scnt[0]:scnt[0] + 4]
                        scnt[0] += 4
                        cx.op('dve', lambda e: e.reciprocal(out=rec, in_=oview[:, :, 64]), [obr], [rn])
                        for j in range(4):
                            qt = qb * 4 + j
                            cp('dve' if j % 2 == 0 else 'act', opair[:, qt, (h % 2) * 64:(h % 2) * 64 + 64],
                               oview[:, j, 0:64], [obr, rn], ['opair'], scale=rec[:, j:j + 1])

                emit_S(0)
                for i in range(len(items)):
                    if i + 1 < len(items):
                        emit_S(i + 1)
                    emit_PV(i)
                    yield

            def otrans(h):
                hp = h // 2
                for blk in range(4):
                    bs = slice(blk * 512, (blk + 1) * 512)
                    for j in range(4):
                        tp(pb[4][:, j * 128:(j + 1) * 128], opair[:, blk * 4 + j, :], ['opair'], ['pb4'],
                           signal=(j == 3))
                    tt_('dve', mixT[:, hp, bs], pb[4][:, :], mixT[:, hp, bs], ALU.mult, ['pb4', 'mixT'], ['mixT'])
                    yield

            for _ in produce(0):
                yield
            for h in range(8):
                ga = attention(h)
                gp = produce(h + 1) if h + 1 < 8 else None
                k = 0
                for _ in ga:
                    yield
                    k += 1
                    if gp is not None and k % 2 == 0:
                        try:
                            next(gp)
                            yield
                        except StopIteration:
                            gp = None
                if gp is not None:
                    for _ in gp:
                        yield
                if h % 2 == 1:
                    for _ in otrans(h):
                        yield

        def out_tile(NT, mix_src, mix_res, xres, xres_res, out_ap, bA, bA_res, bB, bB_res, okey,
                     ysb_t=None, ysb_res='ysb'):
            if ysb_t is None:
                ysb_t = ysb
            for half, (bk, br) in enumerate(((bA, bA_res), (bB, bB_res))):
                for kc in range(8):
                    mm(bk[0:NT, :], mix_src(kc), w_out_b[:, kc, half * 512:(half + 1) * 512], kc == 0, kc == 7,
                       [mix_res, 'w_out_b'], [br])
            c1, r1 = sumsq(bA[0:NT, :], NT, [bA_res])
            c2, r2 = sumsq(bB[0:NT, :], NT, [bB_res])
            tt_('dve', c1[0:NT, :], c1[0:NT, :], c2[0:NT, :], ALU.add, [r1, r2], [r1])
            rstd_from(c1, r1, NT, D)
            for half, (bk, br) in enumerate(((bA, bA_res), (bB, bB_res))):
                hs = slice(half * 512, (half + 1) * 512)
                tt_('dve', ysb_t[0:NT, hs], bk[0:NT, :], gpost_b[0:NT, hs], ALU.mult, [br, 'gpost_b'], [ysb_res])
                stt(ysb_t[0:NT, hs], ysb_t[0:NT, hs], c1[0:NT, :], xres[0:NT, hs], ALU.mult, ALU.add,
                    [ysb_res, r1, xres_res], [ysb_res])
            cx.dma('sp', out_ap, ysb_t[0:NT, :], reads=[ysb_res], key=okey)

        def phase3():
            phase_now[0] = 3
            npair = 4 if sample_done[0] else 2
            ybufs = [(ysb, 'ysb'), (QT.bitcast(F32), 'QT'), (KT.bitcast(F32), 'KT')]
            cx.dma('sp', xst[0][:], xp[0:128, :], writes=['xst0'])
            for tt in range(NTT):
                s = tt % 2
                if tt + 1 < NTT:
                    cx.dma('sp', xst[1 - s][:], xp[(tt + 1) * 128:(tt + 2) * 128, :], writes=['xst%d' % (1 - s)])
                bp = tt % npair
                yb, yr = ybufs[tt % 3]
                out_tile(128, lambda kc, tt=tt: mixT[:, kc, tt * 128:(tt + 1) * 128], 'mixT',
                         xst[s], 'xst%d' % s, y_p[tt * 128:(tt + 1) * 128, :],
                         pb[2 * bp], 'pb%d' % (2 * bp), pb[2 * bp + 1], 'pb%d' % (2 * bp + 1), 'd:o_' + yr,
                         ysb_t=yb, ysb_res=yr)
                yield

        def sample():
            P5, P6, P7 = pb[5], pb[6], pb[7]
            xs_t = xst[1]
            cx.dma('sp', xs_t[0:16, :], xs, writes=['xst1'])
            front_tile(16, xs_t, 'xst1', xs_t, 'xst1', lambda kc: xTs[:, kc, :], 'xTs', [P5, P6], ['pb5', 'pb6'])
            small_proj(16, lambda kc: xTs[:, kc, :], 'xTs', P5, 'pb5', P6, 'pb6',
                       cos_s[:, :], sinm_s[:, :], ['cos_s', 'sinm_s'],
                       ckvn[0], 'ckvn0', kst[0], 'kst', ckv_s[:, :], kpe_s[:, :],
                       cqTs[:, :, :], ckvTs[:, :, :], kpeTs[:, :], 'cTs', [P5, P6], ['pb5', 'pb6'], 0)
            cp('pool', ckvn_sb[:, :], ckvn[0][0:16, :], ['ckvn0'], ['ckvn_sb'])
            cx.op('pool', lambda e: e.memset(kst[0][:, 0:64], 0.0), ['kst'], ['kst'])
            yield
            for h in range(8):
                for kc in range(3):
                    mm(P5[0:64, 0:16], w_uq_b[:, kc, h * 96:h * 96 + 64], cqTs[:, kc, :], kc == 0, kc == 2,
                       ['w_uq_b', 'cTs'], ['pb5'])
                for kc in range(3):
                    mm(P5[0:32, 16:32], w_uq_b[:, kc, h * 96 + 64:h * 96 + 96], cqTs[:, kc, :], kc == 0, kc == 2,
                       ['w_uq_b', 'cTs'], ['pb5'])
                for kc in range(3):
                    mm(P5[0:32, 32:48], w_uqr_b[:, kc, h, 64:96], cqTs[:, kc, :], kc == 0, kc == 2,
                       ['w_uqr_b', 'cTs'], ['pb5'])
                cp('act', qnT_s[:, :], P5[0:64, 0:16], ['pb5'], ['qnT_s'])
                tt_('dve', tmpD[0:32, 112:128], P5[0:32, 16:32], ct_s[:, :], ALU.mult, ['pb5', 'ct_s'], ['tmpD'])
                tt_('dve', tmpD[0:32, 128:144], P5[0:32, 32:48], st_s[:, :], ALU.mult, ['pb5', 'st_s'], ['tmpD'])
                tt_('dve', qpeT[:, :, h, :], tmpD[0:32, 112:128].rearrange("p (b i) -> p b i", b=4),
                    tmpD[0:32, 128:144].rearrange("p (b i) -> p b i", b=4), ALU.add, ['tmpD'], ['qpeT'])
                for kc in range(2):
                    mm(P6[:, kc * 16:(kc + 1) * 16], w_ukT_b[:, h, kc * 128:(kc + 1) * 128], qnT_s[:, :], True, True,
                       ['w_ukT_b', 'qnT_s'], ['pb6'])
                cp('dve', qlatT[:, :, :, h, :], P6[:, 0:32].rearrange("p (k b i) -> p k b i", k=2, b=4),
                   ['pb6'], ['qlatT'])
                yield
            groups = [(b, g) for b in range(4) for g in range(16)]
            NG = len(groups)

            def stage_A(n):
                b, g = groups[n]
                s3 = n % 3
                cx.dma('pool', gck[s3][:, :], cckv, reads=['idx_c'], writes=['gck%d' % s3],
                       indirect=bass.IndirectOffsetOnAxis(ap=idx_c[:, b, g:g + 1], axis=0))
                if g % 8 == 0:
                    cx.dma('pool', gkp[:, :], ckpe, reads=['idx_k'], writes=['gkp'],
                           indirect=bass.IndirectOffsetOnAxis(ap=idx_k[:, b, g // 8:g // 8 + 1], axis=0))

            def stage_B(n):
                b, g = groups[n]
                s3 = n % 3
                s = n % 2
                for r in range(4):
                    for tl in range(2):
                        t = r * 2 + tl
                        for kc in range(2):
                            mm(P5[:, (tl * 2 + kc) * 128:(tl * 2 + kc + 1) * 128],
                               gck[s3][:, t * 256 + kc * 128:t * 256 + (kc + 1) * 128], ident_b[:], True, True,
                               ['gck%d' % s3, 'ident_b'], ['pb5'], signal=(tl == 1 and kc == 1))
                    cp('dve' if r % 2 == 0 else 'act',
                       ckTs[s][:, :, r * 2:r * 2 + 2, :].rearrange("p k t n -> p t k n"),
                       P5[:, :].rearrange("p (t k n) -> p t k n", t=2, k=2), ['pb5'], ['ckTs%d' % s])
                    for tl in range(2):
                        t = r * 2 + tl
                        tk = (g % 8) * 8 + t
                        mm(P6[0:32, 256 + tl * 128:256 + (tl + 1) * 128], gkp[:, tk * 32:(tk + 1) * 32], ident_b[:],
                           True, True, ['gkp', 'ident_b'], ['pb6'], signal=(tl == 1))
                    cp('act' if r % 2 == 0 else 'dve', kpTs[s][:, r * 2:(r + 1) * 2, :],
                       P6[0:32, 256:512].rearrange("p (t n) -> p t n", t=2), ['pb6'], ['kpTs%d' % s])
                    yield

            def stage_C1(n):
                b, g = groups[n]
                s = n % 2
                for t in range(8):
                    o = pb[6][:, t * 32:(t + 1) * 32]
                    for kc in range(2):
                        mm(o, ckTs[s][:, kc, t, :], qlatT[:, kc, b, :, :].rearrange("p h i -> p (h i)"),
                           kc == 0, False, ['ckTs%d' % s, 'qlatT'], ['pb6'], signal=False)
                    mm(o, kpTs[s][:, t, :], qpeT[:, b, :, :].rearrange("p h i -> p (h i)"), False, True,
                       ['kpTs%d' % s, 'qpeT'], ['pb6'], signal=(t == 7))
                cx.op('act', lambda e, s=s: e.activation(out=pTs[s][:, :], in_=pb[6][:, 0:256], func=AF.Exp,
                                                         scale=SCALE), ['pb6'], ['pTs%d' % s])

            def stage_C2(n):
                b, g = groups[n]
                s = n % 2
                s3 = n % 3
                if g == 0:
                    mm(pb[7][0:32, 0:260], ident_b[:, 0:32], zeros260, True, False, ['ident_b', 'w_uqr_b'], ['pb7'],
                       signal=False)
                for t in range(8):
                    mm(pb[7][0:32, 0:256], pTs[s][:, t * 32:(t + 1) * 32], gck[s3][:, t * 256:(t + 1) * 256],
                       False, False, ['pTs%d' % s, 'gck%d' % s3], ['pb7'], signal=False)
                for t in range(8):
                    mm(pb[7][0:32, 256:257], pTs[s][:, t * 32:(t + 1) * 32], ones_b[:, 0:1],
                       False, False, ['pTs%d' % s, 'ones_b'], ['pb7'], signal=(t == 7))

            def batch_tail(b):
                o = pb[6][0:16, 0:32]
                for kc in range(2):
                    mm(o, ckvTs[:, kc, :], qlatT[:, kc, b, :, :].rearrange("p h i -> p (h i)"), kc == 0, False,
                       ['cTs', 'qlatT'], ['pb6'], signal=False)
                mm(o, kpeTs[:, :], qpeT[:, b, :, :].rearrange("p h i -> p (h i)"), False, True,
                   ['cTs', 'qpeT'], ['pb6'])
                cx.op('act', lambda e: e.activation(out=tmpD[0:16, 144:176], in_=pb[6][0:16, 0:32], func=AF.Exp,
                                                    scale=SCALE), ['pb6'], ['tmpD'])
                tt_('dve', pTn[:, :], tmpD[0:16, 144:176], sm_t[:, b, :], ALU.mult, ['tmpD', 'sm_t'], ['pTn'])
                mm(pb[7][0:32, 0:256], pTn[:, :], ckvn_sb[:, :], False, False, ['pTn', 'ckvn_sb'], ['pb7'], signal=False)
                mm(pb[7][0:32, 256:257], pTn[:, :], ones_b[0:16, 0:1], False, True, ['pTn', 'ones_b'], ['pb7'])
                cp('dve', ssm[:, 0:1], pb[7][0:32, 256:257], ['pb7'], ['ssm'])
                cx.op('dve', lambda e: e.reciprocal(out=ssm[:, 1:2], in_=ssm[:, 0:1]), ['ssm'], ['ssm'])
                cp('dve', olat[:, :], pb[7][0:32, 0:256], ['pb7', 'ssm'], ['olat'], scale=ssm[:, 1:2])
                for kc in range(2):
                    tp(P5[:, kc * 32:(kc + 1) * 32], olat[:, kc * 128:(kc + 1) * 128], ['olat'], ['pb5'], signal=(kc == 1))
                cp('dve', olatT[:, :, :, b * 4:(b + 1) * 4], P5[:, 0:64].rearrange("p (k h i) -> p k h i", k=2, h=8),
                   ['pb5'], ['olatT'])

            def feature_major():
                def smm(j, bank, col):
                    for kc in range(8):
                        mm(bank[:, col:col + 16], w_in_b[:, kc, 672 + j * 128:672 + (j + 1) * 128], xTs[:, kc, :],
                           kc == 0, kc == 7, ['w_in_b', 'xTs'], ['pb5'])
                for d in range(4):
                    smm(d, P5, 0)
                    cx.op('act', lambda e, d=d: e.activation(out=mixTs[:, d, :], in_=P5[:, 0:16], func=AF.Silu),
                          ['pb5'], ['mixTs'])
                yield
                for d in range(4):
                    smm(8 + d, P5, 0)
                    smm(12 + d, P5, 16)
                    smm(16 + d, P5, 32)
                    smm(4 + d, P5, 48)
                    cp('act', tmpD[:, 64:80], P5[:, 0:16], ['pb5'], ['tmpD'])
                    uv = uS[:, d, :, :]
                    tt_('dve', uv[:, :, 2:6], P5[:, 16:32].rearrange("p (b i) -> p b i", b=4),
                        tmpD[:, 64:80].rearrange("p (b i) -> p b i", b=4), ALU.mult, ['pb5', 'tmpD'], ['uS'])
                    yv = tmpD[:, 80:96].rearrange("p (b i) -> p b i", b=4)
                    ts_('dve', yv, uv[:, :, 0:4], wconv_c[:, d, 0:1], None, ALU.mult, None, ['uS', 'wconv_c'], ['tmpD'])
                    stt(yv, uv[:, :, 1:5], wconv_c[:, d, 1:2], yv, ALU.mult, ALU.add, ['uS', 'tmpD'], ['tmpD'])
                    stt(yv, uv[:, :, 2:6], wconv_c[:, d, 2:3], yv, ALU.mult, ALU.add, ['uS', 'tmpD'], ['tmpD'])
                    cx.op('act', lambda e: e.activation(out=tmpD[:, 96:112], in_=P5[:, 32:48], func=AF.Silu),
                          ['pb5'], ['tmpD'])
                    tt_('dve', tmpD[:, 80:96], P5[:, 48:64], tmpD[:, 80:96], ALU.mult, ['pb5', 'tmpD'], ['tmpD'])
                    tt_('dve', mixTs[:, 4 + d, :], tmpD[:, 80:96], tmpD[:, 96:112], ALU.mult, ['tmpD'], ['mixTs'])
                    yield
                cx.op('dve', lambda e: e.tensor_copy(out=tmpD[:, 224:256].rearrange("p (c b k) -> p c b k", c=4, b=4),
                                                     in_=uS[:, :, :, 4:6]), reads=['uS'], writes=['tmpD'])
                cx.dma('sp', conv_s.rearrange("p c b k -> p (c b k)"), tmpD[:, 224:256], reads=['tmpD'], key='d:o_convs')

            stage_A(0)
            stage_A(1)
            for _ in stage_B(0):
                yield
            for _ in feature_major():
                yield
            for n in range(NG):
                stage_C1(n)
                yield
                if n + 1 < NG:
                    for _ in stage_B(n + 1):
                        yield
                if n + 2 < NG:
                    stage_A(n + 2)
                stage_C2(n)
                if groups[n][1] == 15:
                    batch_tail(groups[n][0])
                yield
            for hp in range(4):
                n = 0
                for hh in range(2):
                    h = hp * 2 + hh
                    for kc in range(2):
                        mm(P5[:, 0:16], w_uvp_b[:, kc, h, :], olatT[:, kc, h, :], n == 0, n == 3,
                           ['w_uvp_b', 'olatT'], ['pb5'])
                        n += 1
                tt_('dve', mixTs[:, hp, :], P5[:, 0:16], mixTs[:, hp, :], ALU.mult, ['pb5', 'mixTs'], ['mixTs'])
            yield
            assert wout_state.get('final'), 'w_out not fully issued before the sample epilogue'
            cx.dma('sp', xs_t[0:16, :], xs, writes=['xst1'])
            out_tile(16, lambda kc: mixTs[:, kc, :], 'mixTs', xs_t, 'xst1', y_s[:, :], P5, 'pb5', P6, 'pb6', 'd:o_ysb')
            sample_done[0] = True
            yield

        for _ in prologue():
            pass

        def chain(*gens):
            for g_ in gens:
                for _ in g_:
                    yield

        P = chain(phase1(), phase2(), phase3())
        S = sample()
        next(S)
        p_alive, s_alive = True, True
        acc = 0.0
        RATIOS = {1: 0.85, 2: 0.85, 3: 1.0}
        while p_alive or s_alive:
            if p_alive:
                try:
                    next(P)
                except StopIteration:
                    p_alive = False
            acc += RATIOS[phase_now[0]]
            while s_alive and (acc >= 1.0 or not p_alive):
                acc -= 1.0
                try:
                    next(S)
                except StopIteration:
                    s_alive = False
        cx.finish('sp')
        build_nc.stats = dict(nwaits=cx.nwaits, cnt=dict(cx.cnt), nsem=len(cx.dsem) + 5, late=off[0])
    return nc


_NC = None


def _rope_tables():
    inv = (np.float32(10000.0) ** (-(np.arange(0, 32, 2, dtype=np.float32)) / np.float32(32))).astype(np.float32)

    def cs(pos):
        ang = (pos.astype(np.float32)[:, None] * inv[None, :]).astype(np.float32)
        ang = np.concatenate([ang, ang], axis=-1).astype(np.float64)
        return np.cos(ang).astype(np.float32), np.sin(ang).astype(np.float32)

    cp_, sp_ = cs(np.arange(SEQ))
    sgn = np.concatenate([-np.ones(16, np.float32), np.ones(16, np.float32)])
    r_cos = np.ascontiguousarray(cp_.reshape(NTT, 128, 32).transpose(1, 0, 2))
    r_sinm = np.ascontiguousarray((sp_ * sgn).reshape(NTT, 128, 32).transpose(1, 0, 2))
    r_ct = np.ones((96, SEQ), np.float32)
    r_st = np.zeros((96, SEQ), np.float32)
    r_ct[64:96] = cp_.T
    r_st[64:96] = sp_.T
    pos_s = PAST + (np.arange(16) % 4)
    cs_, ss_ = cs(pos_s)
    smask = np.zeros((16, 4, 32), np.float32)
    for b in range(4):
        for k in range(4):
            for h in range(8):
                for i in range(4):
                    if k <= i:
                        smask[b * 4 + k, b, h * 4 + i] = 1.0
    return dict(r_cos=r_cos, r_sinm=r_sinm, r_ct=r_ct, r_st=r_st,
                rs_cos=cs_, rs_sinm=(ss_ * sgn).astype(np.float32),
                rs_ct=np.ascontiguousarray(cs_.T), rs_st=np.ascontiguousarray(ss_.T), smask=smask)


def kernel(x_prompt, x_sample, cache_ckv, cache_kpe, state_conv, page_table,
           g_pre, w_in, g_qnorm, w_uq, g_kvnorm, w_ukv, w_conv, w_out, g_post):
    global _NC
    if _NC is None:
        _NC = build_nc()
    nc = _NC
    f = lambda a: np.ascontiguousarray(np.asarray(a, dtype=np.float32))
    x_prompt, x_sample = f(x_prompt), f(x_sample)
    tabs = _rope_tables()
    cckv = f(cache_ckv).reshape(NPHYS * 16, 2048)
    ckpe = f(cache_kpe).reshape(NPHYS * 2, 2048)
    pt = np.asarray(page_table, dtype=np.int32)
    sc = f(state_conv)[0]
    shared = dict(
        cckv=cckv, ckpe=ckpe,
        gpre=np.ascontiguousarray(f(g_pre)[0].reshape(8, 128).T),
        w_in=f(w_in)[0],
        gq=np.ascontiguousarray(f(g_qnorm)[0].reshape(3, 128).T),
        w_uq=f(w_uq)[0],
        gkv=np.ascontiguousarray(np.broadcast_to(f(g_kvnorm)[0][None, :], (128, 256))),
        w_ukv=f(w_ukv)[0],
        wconv=np.ascontiguousarray(f(w_conv)[0].reshape(3, 4, 128).transpose(2, 1, 0)),
        w_out=f(w_out)[0],
        gpost=np.ascontiguousarray(np.broadcast_to(f(g_post)[0][None, :], (128, D))),
        **tabs)
    in_maps = []
    for c in range(8):
        m = dict(shared)
        m["xp"] = x_prompt[c]
        m["xs"] = np.ascontiguousarray(x_sample[4 * c:4 * c + 4].reshape(16, D))
        m["ptab"] = np.ascontiguousarray(pt[4 * c:4 * c + 4].T)
        scc = sc[4 * c:4 * c + 4]
        m["sconv"] = np.ascontiguousarray(scc.reshape(4, 2, 4, 128).transpose(3, 2, 0, 1))
        in_maps.append(m)
    res = run_bass_kernel_spmd(nc, in_maps, core_ids=list(range(8)))
    R = res.results
    g = lambda k, c: np.asarray(R[c][k], dtype=np.float32)
    y_prompt = np.stack([g("y_p", c) for c in range(8)])
    y_sample = np.concatenate([g("y_s", c).reshape(4, 4, D) for c in range(8)])
    ckv_p = np.stack([g("ckv_p", c) for c in range(8)])[None]
    kpe_p = np.stack([g("kpe_p", c) for c in range(8)])[None]
    conv_p = np.stack([g("conv_p", c).transpose(2, 1, 0).reshape(2, 512) for c in range(8)])[None]
    ckv_s = np.concatenate([g("ckv_s", c).reshape(4, 4, 256) for c in range(8)])[None]
    kpe_s = np.concatenate([g("kpe_s", c).reshape(4, 4, 32) for c in range(8)])[None]
    conv_s = np.concatenate([g("conv_s", c).transpose(2, 3, 1, 0).reshape(4, 2, 512) for c in range(8)])[None]
    return (y_prompt, y_sample, ckv_p, kpe_p, conv_p, ckv_s, kpe_s, conv_s)
```

```python
import numpy as np
from contextlib import ExitStack
import concourse.bass as bass
import concourse.mybir as mybir
from concourse.bass_utils import run_bass_kernel_spmd

F32 = mybir.dt.float32
BF16 = mybir.dt.bfloat16
I32 = mybir.dt.int32
AF = mybir.ActivationFunctionType
ALU = mybir.AluOpType

D = 1024
SEQ = 2048
NTT = 16
DIN = 3232
PAST = 16384
NPHYS = 5120
EPS = 1e-6
SCALE = 96 ** -0.5
NEG = -30000.0


class Ctx:
    def __init__(self, nc, stack, same_engine_sync=True):
        self.nc = nc
        self.stack = stack
        self.E = {'pe': nc.tensor, 'act': nc.scalar, 'dve': nc.vector, 'pool': nc.gpsimd, 'sp': nc.sync}
        self.sem = {}
        self.cnt = {}
        for e in self.E:
            self.sem[e] = stack.enter_context(nc.semaphore("s_" + e))
            self.cnt[e] = 0
        self.seen = {e: {} for e in self.E}
        self.res = {}
        self.dsem = {}
        self.dcnt = {}
        self.same = same_engine_sync
        self.nwaits = 0

    def _r(self, name):
        if name not in self.res:
            self.res[name] = {'w': None, 'r': {}}
        return self.res[name]

    def _deps(self, reads, writes):
        deps = []
        for r in reads:
            w = self._r(r)['w']
            if w is not None:
                deps.append(w)
        for wn in writes:
            rr = self._r(wn)
            if rr['w'] is not None:
                deps.append(rr['w'])
            deps.extend(rr['r'].items())
        return deps

    def _wait(self, e, deps):
        eng = self.E[e]
        best = {}
        for (key, val) in deps:
            if key == e and (e == 'pe' or not self.same):
                continue
            if val > best.get(key, 0):
                best[key] = val
        for key, val in best.items():
            if self.seen[e].get(key, 0) >= val:
                continue
            s = self.sem[key] if key in self.sem else self.dsem[key]
            eng.wait_ge(s, val)
            self.nwaits += 1
            self.seen[e][key] = val

    def _record(self, tok, reads, writes):
        for r in reads:
            d = self._r(r)['r']
            if tok[1] > d.get(tok[0], 0):
                d[tok[0]] = tok[1]
        for wn in writes:
            self.res[wn] = {'w': tok, 'r': {}}

    def op(self, e, fn, reads=(), writes=(), signal=True):
        self._wait(e, self._deps(reads, writes))
        inst = fn(self.E[e])
        if signal:
            self.cnt[e] += 1
            inst.then_inc(self.sem[e], 1)
            tok = (e, self.cnt[e])
        else:
            tok = (e, self.cnt[e] + 1)
        self._record(tok, reads, writes)
        return inst

    def dma(self, q, out, in_, reads=(), writes=(), key=None, indirect=None):
        if key is None:
            key = 'd:' + (writes[0] if writes else reads[0])
        if key not in self.dsem:
            self.dsem[key] = self.stack.enter_context(self.nc.semaphore("s_" + key.replace(':', '_')))
            self.dcnt[key] = 0
        self._wait(q, self._deps(reads, writes))
        if indirect is None:
            inst = self.E[q].dma_start(out=out, in_=in_)
        else:
            inst = self.E[q].indirect_dma_start(out=out, out_offset=None, in_=in_, in_offset=indirect)
        self.dcnt[key] += 16
        inst.then_inc(self.dsem[key], 16)
        self._record((key, self.dcnt[key]), reads, writes)
        return inst

    def finish(self, e='sp'):
        deps = [(k, v) for k, v in self.dcnt.items()]
        for k in self.E:
            if k != e and self.cnt[k] > 0:
                deps.append((k, self.cnt[k]))
        self._wait(e, deps)


def build_nc():
    nc = bass.Bass("TRN2", target_bir_lowering=False)

    def din(name, shape, dt=F32):
        return nc.dram_tensor(name, list(shape), dt, kind="ExternalInput").ap()

    def dout(name, shape, dt=F32):
        return nc.dram_tensor(name, list(shape), dt, kind="ExternalOutput").ap()

    xp = din("xp", [SEQ, D])
    xs = din("xs", [16, D])
    cckv = din("cckv", [NPHYS * 16, 2048])
    ckpe = din("ckpe", [NPHYS * 2, 2048])
    sconv = din("sconv", [128, 4, 4, 2])
    ptab = din("ptab", [128, 4], I32)
    gpre = din("gpre", [128, 8])
    w_in = din("w_in", [D, DIN])
    gq = din("gq", [128, 3])
    w_uq = din("w_uq", [384, 768])
    gkv = din("gkv", [128, 256])
    w_ukv = din("w_ukv", [256, 1024])
    wconv = din("wconv", [128, 4, 3])
    w_out = din("w_out", [D, D])
    gpost = din("gpost", [128, D])
    r_cos = din("r_cos", [128, NTT, 32])
    r_sinm = din("r_sinm", [128, NTT, 32])
    r_ct = din("r_ct", [96, SEQ])
    r_st = din("r_st", [96, SEQ])
    rs_cos = din("rs_cos", [16, 32])
    rs_sinm = din("rs_sinm", [16, 32])
    rs_ct = din("rs_ct", [32, 16])
    rs_st = din("rs_st", [32, 16])
    smask = din("smask", [16, 4, 32])

    y_p = dout("y_p", [SEQ, D])
    y_s = dout("y_s", [16, D])
    ckv_p = dout("ckv_p", [SEQ, 256])
    kpe_p = dout("kpe_p", [SEQ, 32])
    conv_p = dout("conv_p", [128, 4, 2])
    ckv_s = dout("ckv_s", [16, 256])
    kpe_s = dout("kpe_s", [16, 32])
    conv_s = dout("conv_s", [128, 4, 4, 2])

    with ExitStack() as st:
        cx = Ctx(nc, st)

        def T(name, shape, dt):
            return st.enter_context(nc.sbuf_tensor(name, list(shape), dt))

        pb = [st.enter_context(nc.psum_tensor("pb%d" % i, [128, 512], F32)) for i in range(8)]

        wbig = T("wbig", [128, 8 * DIN], BF16)
        w_in_b = wbig[:, :].rearrange("p (k n) -> p k n", k=8)
        ident_f = T("ident_f", [128, 128], F32)
        ident_b = T("ident_b", [128, 128], BF16)
        maskneg = T("maskneg", [128, 128], BF16)
        ones_b = T("ones_b", [128, 2], BF16)
        eps_t = T("eps_t", [128, 1], F32)
        gpre_c = T("gpre_c", [128, 8], F32)
        gq_c = T("gq_c", [128, 3], F32)
        gkv_b = T("gkv_b", [128, 256], F32)
        wconv_c = T("wconv_c", [128, 4, 3], F32)
        w_uq_b = T("w_uq_b", [128, 3, 800], BF16)
        w_uqr_b = T("w_uqr_b", [128, 3, 8, 96], BF16)
        w_ukv_b = T("w_ukv_b", [128, 2, 1024], BF16)
        w_ukT_b = T("w_ukT_b", [64, 8, 256], BF16)
        w_uvp_b = T("w_uvp_b", [128, 2, 8, 128], BF16)
        cos_t = T("cos_t", [128, NTT, 32], F32)
        sinm_t = T("sinm_t", [128, NTT, 32], F32)
        xst = [T("xst%d" % i, [128, D], F32) for i in range(2)]
        sqs = T("sqs", [128, D], BF16)
        stat = T("stat", [128, 384], F32)
        xT_b = T("xT_b", [128, 8, 512], BF16)
        cqT_b = T("cqT_b", [128, 3, SEQ], BF16)
        ckvT_b = T("ckvT_b", [128, 2, SEQ], BF16)
        kpeT_b = T("kpeT_b", [96, SEQ], BF16)
        mixT = T("mixT", [128, 8, SEQ], BF16)
        uT = [T("uT%d" % d, [128, 514], F32) for d in range(4)]
        tmpA = T("tmpA", [128, 512], F32)
        tmpB = T("tmpB", [128, 512], F32)
        tmpC = T("tmpC", [128, 512], F32)
        tmpD = T("tmpD", [128, 512], F32)
        cqn = T("cqn", [128, 384], F32)
        ckvn = [T("ckvn%d" % i, [128, 256], F32) for i in range(2)]
        kst1 = T("kst", [128, 96], F32)
        kst = [kst1, kst1]
        xTs = T("xTs", [128, 8, 16], BF16)
        cqTs = T("cqTs", [128, 3, 16], BF16)
        ckvTs = T("ckvTs", [128, 2, 16], BF16)
        kpeTs = T("kpeTs", [32, 16], BF16)
        ckvn_sb = T("ckvn_sb", [16, 256], BF16)
        mixTs = T("mixTs", [128, 8, 16], BF16)
        uS = T("uS", [128, 4, 4, 6], F32)
        sm_t = T("sm_t", [16, 4, 32], F32)
        cos_s = T("cos_s", [16, 32], F32)
        sinm_s = T("sinm_s", [16, 32], F32)
        ct_s = T("ct_s", [32, 16], F32)
        st_s = T("st_s", [32, 16], F32)
        qlatT = T("qlatT", [128, 2, 4, 8, 4], BF16)
        qpeT = T("qpeT", [64, 4, 8, 4], BF16)
        qnT_s = T("qnT_s", [64, 16], BF16)
        olatT = T("olatT", [128, 2, 8, 16], BF16)
        idx_t = T("idx_t", [128, 4], I32)
        idx_c = T("idx_c", [128, 4, 16], I32)
        idx_k = T("idx_k", [128, 4, 2], I32)
        gck = [T("gck%d" % i, [128, 2048], BF16) for i in range(3)]
        gkp = T("gkp", [128, 2048], BF16)
        ckTs = [T("ckTs%d" % i, [128, 2, 8, 128], BF16) for i in range(2)]
        kpTs = [T("kpTs%d" % i, [64, 4, 128], BF16) for i in range(2)]
        pTs = [T("pTs%d" % i, [128, 256], BF16) for i in range(2)]
        pTn = T("pTn", [16, 32], BF16)
        ssm = T("ssm", [32, 8], F32)
        cpst = T("cpst", [128, 4, 2], F32)

        LA = {}
        off = [0]

        def late(name, parts, nelem_bf16):
            a = off[0]
            off[0] += nelem_bf16
            assert off[0] <= 8 * DIN
            LA[name] = wbig[0:parts, a:a + nelem_bf16]
            return LA[name]

        w_out_b = late("w_out", 128, 8 * D).rearrange("p (k n) -> p k n", k=8)
        QT = late("QT", 128, SEQ)
        KT = late("KT", 128, SEQ)
        Vh = late("Vh", 128, 16 * 65 + 16)[:, 0:16 * 65].rearrange("p (k v) -> p k v", v=65)
        pT = [late("pT%d" % i, 128, 512) for i in range(3)]
        opair = late("opair", 128, 2 * 16 * 128).bitcast(F32).rearrange("p (q v) -> p q v", v=128)
        gpost_b = late("gpost", 128, 2 * D).bitcast(F32)
        ysb = late("ysb", 128, 2 * 1024).bitcast(F32)
        ct_blk = late("ct_blk", 96, 1024).bitcast(F32)
        st_blk = late("st_blk", 96, 1024).bitcast(F32)
        olat = tmpD[0:32, 256:512]
        LATE = 'wbig'
        xTflat = xT_b[:, :, :].rearrange("p k n -> p (k n)")
        QT1 = xTflat[:, 0:SEQ]
        Vh1 = xTflat[:, SEQ:SEQ + 16 * 65].rearrange("p (k v) -> p k v", v=65)
        KT1 = xst[0][:, :].bitcast(BF16)
        wout_state = {}
        sample_done = [False]
        phase_now = [1]

        zeros260 = w_uqr_b[:, 0, 0:5, 0:52]
        def mm(out, lhsT, rhs, start, stop, reads, writes, signal=None):
            if signal is None:
                signal = stop
            return cx.op('pe', lambda e: e.matmul(out, lhsT=lhsT, rhs=rhs, start=start, stop=stop),
                         reads=reads, writes=writes, signal=signal)

        def tp(out, in_, reads, writes, signal=True):
            return cx.op('pe', lambda e: e.transpose(out=out, in_=in_, identity=ident_f[0:in_.shape[0], 0:in_.shape[0]]),
                         reads=list(reads) + ['ident_f'], writes=writes, signal=signal)

        def cp(eng, out, in_, reads, writes, scale=None):
            if eng == 'act':
                if scale is None:
                    return cx.op('act', lambda e: e.activation(out=out, in_=in_, func=AF.Copy), reads, writes)
                return cx.op('act', lambda e: e.activation(out=out, in_=in_, func=AF.Identity, scale=scale), reads, writes)
            if scale is None:
                return cx.op(eng, lambda e: e.tensor_copy(out=out, in_=in_), reads, writes)
            return cx.op(eng, lambda e: e.tensor_scalar(out=out, in0=in_, scalar1=scale, scalar2=None, op0=ALU.mult),
                         reads, writes)

        def tt_(eng, out, in0, in1, op, reads, writes):
            return cx.op(eng, lambda e: e.tensor_tensor(out=out, in0=in0, in1=in1, op=op), reads, writes)

        def ts_(eng, out, in0, s1, s2, op0, op1, reads, writes):
            if s2 is None:
                return cx.op(eng, lambda e: e.tensor_scalar(out=out, in0=in0, scalar1=s1, scalar2=None, op0=op0),
                             reads, writes)
            return cx.op(eng, lambda e: e.tensor_scalar(out=out, in0=in0, scalar1=s1, scalar2=s2, op0=op0, op1=op1),
                         reads, writes)

        def stt(out, in0, scalar, in1, op0, op1, reads, writes):
            return cx.op('dve', lambda e: e.scalar_tensor_tensor(out=out, in0=in0, scalar=scalar, in1=in1,
                                                                 op0=op0, op1=op1), reads, writes)

        scnt = [0]

        def newstat():
            i = scnt[0]
            scnt[0] += 1
            assert i < 380
            return stat[:, i:i + 1], 'st%d' % i

        def sumsq(in_, nparts, reads):
            col, rn = newstat()
            n = in_.shape[-1]
            cx.op('act', lambda e: e.activation(out=sqs[0:nparts, 0:n], in_=in_, func=AF.Square,
                                                accum_out=col[0:nparts, :]),
                  reads=list(reads), writes=['sqs', rn])
            return col, rn

        def rstd_from(col, rn, nparts, n):
            cx.op('act', lambda e: e.activation(out=col[0:nparts, :], in_=col[0:nparts, :], func=AF.Sqrt,
                                                bias=eps_t[0:nparts, :], scale=1.0 / n), [rn, 'eps_t'], [rn])
            cx.op('dve', lambda e: e.reciprocal(out=col[0:nparts, :], in_=col[0:nparts, :]), [rn], [rn])

        def prologue():
            cx.dma('sp', gpre_c[:], gpre, writes=['gpre_c'])
            cx.dma('sp', gq_c[:], gq, writes=['gq_c'])
            cx.dma('sp', gkv_b[:], gkv, writes=['gkv_b'])
            cx.dma('sp', wconv_c[:], wconv, writes=['wconv_c'])
            cx.dma('sp', cos_t[:], r_cos, writes=['cos_t'])
            cx.dma('sp', sinm_t[:], r_sinm, writes=['sinm_t'])
            cx.dma('sp', cos_s[:], rs_cos, writes=['cos_s'])
            cx.dma('sp', sinm_s[:], rs_sinm, writes=['sinm_s'])
            cx.dma('sp', ct_s[:], rs_ct, writes=['ct_s'])
            cx.dma('sp', st_s[:], rs_st, writes=['st_s'])
            cx.dma('sp', sm_t[:], smask, writes=['sm_t'])
            cx.dma('sp', idx_t[:], ptab, writes=['idx_t'])
            cx.dma('sp', tmpD[:, 192:224], sconv.rearrange("p c b k -> p (c b k)"), writes=['tmpD'])
            cx.op('dve', lambda e: e.tensor_copy(out=uS[:, :, :, 0:2],
                                                 in_=tmpD[:, 192:224].rearrange("p (c b k) -> p c b k", c=4, b=4)),
                  reads=['tmpD'], writes=['uS'])
            cx.op('pool', lambda e: e.memset(ident_f[:], 0.0), writes=['ident_f'])
            cx.op('pool', lambda e: e.affine_select(out=ident_f[:], in_=ident_f[:], pattern=[[-1, 128]],
                                                    compare_op=ALU.not_equal, fill=1.0, base=0, channel_multiplier=1),
                  reads=['ident_f'], writes=['ident_f'])
            cx.op('pool', lambda e: e.tensor_copy(out=ident_b[:], in_=ident_f[:]), reads=['ident_f'], writes=['ident_b'])
            cx.op('pool', lambda e: e.memset(tmpA[:, 0:128], 0.0), writes=['tmpA'])
            cx.op('pool', lambda e: e.affine_select(out=tmpA[:, 0:128], in_=tmpA[:, 0:128], pattern=[[1, 128]],
                                                    compare_op=ALU.is_ge, fill=NEG, base=0, channel_multiplier=-1),
                  reads=['tmpA'], writes=['tmpA'])
            cx.op('pool', lambda e: e.tensor_copy(out=maskneg[:], in_=tmpA[:, 0:128]), reads=['tmpA'], writes=['maskneg'])
            cx.op('pool', lambda e: e.memset(ones_b[:], 1.0), writes=['ones_b'])
            cx.op('pool', lambda e: e.memset(eps_t[:], EPS), writes=['eps_t'])
            cx.op('pool', lambda e: e.memset(kst[0][:], 0.0), writes=['kst'])
            for d in range(4):
                cx.op('pool', lambda e, d=d: e.memset(uT[d][:, 0:2], 0.0), writes=['uT%d' % d])
            cx.op('pool', lambda e: e.memset(w_uqr_b[:], 0.0), writes=['w_uqr_b'])
            cx.op('pool', lambda e: e.memset(w_uq_b[:, :, 768:800], 0.0), writes=['w_uq_b'])
            cx.op('pool', lambda e: e.memset(w_uvp_b[:], 0.0), writes=['w_uvp_b'])
            for b in range(4):
                for g in range(16):
                    ts_('dve', idx_c[:, b, g:g + 1], idx_t[:, b:b + 1], 16, g, ALU.mult, ALU.add, ['idx_t'], ['idx_c'])
                for hf in range(2):
                    ts_('dve', idx_k[:, b, hf:hf + 1], idx_t[:, b:b + 1], 2, hf, ALU.mult, ALU.add, ['idx_t'], ['idx_k'])
            for (c0, c1, rn_) in ((0, 672, 'w_in_a'), (672, 2016, 'w_in_b'), (2016, DIN, 'w_in_b')):
                for kc in range(8):
                    cx.dma('pool', w_in_b[:, kc, c0:c1], w_in[kc * 128:(kc + 1) * 128, c0:c1],
                           writes=[rn_], key='d:' + rn_ + str(c0))
            stg = mixT[:, :, :].rearrange("p k n -> p (k n)").bitcast(F32)
            for kc in range(3):
                cx.dma('sp', stg[:, kc * 768:(kc + 1) * 768], w_uq[kc * 128:(kc + 1) * 128, :], writes=['stg_q%d' % kc])
            for kc in range(2):
                cx.dma('sp', stg[:, 2304 + kc * 1024:2304 + (kc + 1) * 1024], w_ukv[kc * 128:(kc + 1) * 128, :],
                       writes=['stg_k%d' % kc])
            for kc in range(3):
                sbuf = stg[:, kc * 768:(kc + 1) * 768]
                sres = 'stg_q%d' % kc
                ts_('dve', w_uq_b[:, kc, 0:768], sbuf, gq_c[:, kc:kc + 1], None, ALU.mult, None,
                    [sres, 'gq_c', 'w_uq_b'], ['w_uq_b'])
                src = sbuf.rearrange("p (h x) -> p h x", x=96)
                ts_('dve', w_uqr_b[:, kc, :, 64:80], src[:, :, 80:96], gq_c[:, kc:kc + 1], -1.0, ALU.mult, ALU.mult,
                    [sres, 'gq_c', 'w_uqr_b'], ['w_uqr_b'])
                ts_('dve', w_uqr_b[:, kc, :, 80:96], src[:, :, 64:80], gq_c[:, kc:kc + 1], None, ALU.mult, None,
                    [sres, 'gq_c', 'w_uqr_b'], ['w_uqr_b'])
            for kc in range(2):
                sbuf = stg[:, 2304 + kc * 1024:2304 + (kc + 1) * 1024]
                sres = 'stg_k%d' % kc
                cp('dve', w_ukv_b[:, kc, :], sbuf, [sres], ['w_ukv_b'])
                srcv = sbuf.rearrange("p (h x) -> p h x", x=128)
                hv = w_uvp_b[:, kc, :, :].rearrange("p (a two) c -> p a two c", two=2)
                sv = srcv.rearrange("p (a two) x -> p a two x", two=2)
                for par in range(2):
                    cp('dve', hv[:, :, par, par * 64:par * 64 + 64], sv[:, :, par, 64:128], [sres, 'w_uvp_b'], ['w_uvp_b'])
                for hq in range(2):
                    for hl in range(4):
                        h = hq * 4 + hl
                        tp(pb[5][0:64, hl * 128:(hl + 1) * 128], sbuf[:, h * 128:h * 128 + 64], [sres], ['pb5'],
                           signal=(hl == 3))
                    cp('dve', w_ukT_b[:, hq * 4:(hq + 1) * 4, kc * 128:(kc + 1) * 128],
                       pb[5][0:64, :].rearrange("p (a c) -> p a c", a=4), ['pb5'], ['w_ukT_b'])
            cx.op('dve', lambda e: e.memset(mixT[:, 0, 0:2], 0.0),
                  writes=['mixT', 'stg_q0', 'stg_q1', 'stg_q2', 'stg_k0', 'stg_k1'])
            yield

        def front_tile(NT, xt, xt_res, xn, xn_res, xT_dst, xT_res, pbx, pbx_res, stage='AB'):
            if 'A' in stage:
                col, rn = sumsq(xt[0:NT, :], NT, [xt_res])
                rstd_from(col, rn, NT, D)
                cp('act', xn[0:NT, :], xt[0:NT, :], [xt_res, rn], [xn_res], scale=col[0:NT, :])
            if 'B' in stage:
                for half in range(2):
                    bank = pbx[half]
                    for j in range(4):
                        kc = half * 4 + j
                        tp(bank[:, j * NT:(j + 1) * NT], xn[0:NT, kc * 128:(kc + 1) * 128], [xn_res], [pbx_res[half]],
                           signal=(j == 3))
                    for j in range(4):
                        kc = half * 4 + j
                        cp('dve' if j % 2 == 0 else 'act', xT_dst(kc), bank[:, j * NT:(j + 1) * NT],
                           [pbx_res[half], 'gpre_c'], [xT_res], scale=gpre_c[:, kc:kc + 1])

        def small_proj(NT, xT_src, xT_res, bA, bA_res, bB, bB_res, cos_ap, sinm_ap, tbl_res,
                       ckvn_t, ckvn_res, kst_t, kst_res, out_ckv, out_kpe,
                       cqT_dst, ckvT_dst, kpeT_dst, dst_res, pbt, pbt_res, kpe_col0, stage='ABC'):
            c0 = kpe_col0
            if 'A' in stage:
                for kc in range(8):
                    mm(bA[0:NT, 0:384], xT_src(kc), w_in_b[:, kc, 0:384], kc == 0, kc == 7, [xT_res, 'w_in_a'], [bA_res])
                for kc in range(8):
                    mm(bB[0:NT, 0:288], xT_src(kc), w_in_b[:, kc, 384:672], kc == 0, kc == 7, [xT_res, 'w_in_a'], [bB_res])
            if 'B' in stage:
                cq_col, cq_rn = sumsq(bA[0:NT, 0:384], NT, [bA_res])
                kv_col, kv_rn = sumsq(bB[0:NT, 0:256], NT, [bB_res])
                rstd_from(cq_col, cq_rn, NT, 384)
                rstd_from(kv_col, kv_rn, NT, 256)
                cp('act', cqn[0:NT, :], bA[0:NT, 0:384], [bA_res, cq_rn], ['cqn'], scale=cq_col[0:NT, :])
                stt(ckvn_t[0:NT, :], bB[0:NT, 0:256], kv_col[0:NT, :], gkv_b[0:NT, :], ALU.mult, ALU.mult,
                    [bB_res, kv_rn, 'gkv_b'], [ckvn_res])
                k = bB[0:NT, 256:288]
                tt_('dve', tmpD[0:NT, 0:32], k, cos_ap, ALU.mult, [bB_res] + list(tbl_res), ['tmpD'])
                tt_('dve', tmpD[0:NT, 32:48], bB[0:NT, 272:288], sinm_ap[:, 0:16], ALU.mult, [bB_res] + list(tbl_res), ['tmpD'])
                tt_('dve', tmpD[0:NT, 48:64], bB[0:NT, 256:272], sinm_ap[:, 16:32], ALU.mult, [bB_res] + list(tbl_res), ['tmpD'])
                tt_('dve', kst_t[0:NT, c0:c0 + 32], tmpD[0:NT, 0:32], tmpD[0:NT, 32:64], ALU.add, ['tmpD'], [kst_res])
                cx.dma('sp', out_ckv, ckvn_t[0:NT, :], reads=[ckvn_res], key='d:o_' + ckvn_res)
                cx.dma('sp', out_kpe, kst_t[0:NT, c0:c0 + 32], reads=[kst_res], key='d:o_' + kst_res)
            if 'C' in stage:
                for j in range(3):
                    tp(pbt[0][:, j * NT:(j + 1) * NT], cqn[0:NT, j * 128:(j + 1) * 128], ['cqn'], [pbt_res[0]], signal=(j == 2))
                for j in range(2):
                    tp(pbt[1][:, j * NT:(j + 1) * NT], ckvn_t[0:NT, j * 128:(j + 1) * 128], [ckvn_res], [pbt_res[1]], signal=False)
                tp(pbt[1][0:c0 + 32, 2 * NT:3 * NT], kst_t[0:NT, 0:c0 + 32], [kst_res], [pbt_res[1]])
                cp('act', cqT_dst, pbt[0][:, 0:3 * NT].rearrange("p (j t) -> p j t", j=3), [pbt_res[0]], [dst_res])
                cp('dve', ckvT_dst, pbt[1][:, 0:2 * NT].rearrange("p (j t) -> p j t", j=2), [pbt_res[1]], [dst_res])
                cp('dve', kpeT_dst, pbt[1][c0:c0 + 32, 2 * NT:3 * NT], [pbt_res[1]], [dst_res])

        def phase1():
            bankrot = [0]

            def load_x(tt):
                cx.dma('sp', xst[tt % 2][:], xp[tt * 128:(tt + 1) * 128, :], writes=['xst%d' % (tt % 2)])

            def fr(tt, stage):
                s_, t4 = tt % 2, tt % 4
                front_tile(128, xst[s_], 'xst%d' % s_, xst[s_], 'xst%d' % s_,
                           lambda kc: xT_b[:, kc, t4 * 128:(t4 + 1) * 128], 'xT_b',
                           [pb[0], pb[1]], ['pb0', 'pb1'], stage=stage)

            def sm(tt, stage):
                s_, t4 = tt % 2, tt % 4
                tok = slice(tt * 128, (tt + 1) * 128)
                small_proj(128, lambda kc: xT_b[:, kc, t4 * 128:(t4 + 1) * 128], 'xT_b',
                           pb[2], 'pb2', pb[3], 'pb3',
                           cos_t[:, tt, :], sinm_t[:, tt, :], ['cos_t', 'sinm_t'],
                           ckvn[s_], 'ckvn%d' % s_, kst[0], 'kst',
                           ckv_p[tok, :], kpe_p[tok, :],
                           cqT_b[:, :, tok], ckvT_b[:, :, tok], kpeT_b[64:96, tok], 'cT_b',
                           [pb[0], pb[1]], ['pb0', 'pb1'], 64, stage=stage)

            load_x(0)
            load_x(1)
            fr(0, 'A')
            for blk in range(4):
                for t4 in range(4):
                    tt = blk * 4 + t4
                    if tt + 1 < NTT:
                        fr(tt + 1, 'A')
                    fr(tt, 'B')
                    if tt + 2 < NTT:
                        load_x(tt + 2)
                    yield
                    sm(tt, 'A')
                    if t4 > 0:
                        sm(tt - 1, 'C')
                    sm(tt, 'B')
                    yield
                sm(blk * 4 + 3, 'C')
                yield
                bs = slice(blk * 512, (blk + 1) * 512)

                def bigmm(j):
                    bi = bankrot[0] % 5
                    bankrot[0] += 1
                    for kc in range(8):
                        mm(pb[bi][:, :], w_in_b[:, kc, 672 + j * 128:672 + (j + 1) * 128], xT_b[:, kc, :],
                           kc == 0, kc == 7, ['w_in_b', 'xT_b'], ['pb%d' % bi])
                    return pb[bi], 'pb%d' % bi

                for d in range(4):
                    bk, br = bigmm(d)
                    cx.op('act', lambda e, bk=bk, d=d: e.activation(out=mixT[:, d, bs], in_=bk[:, :], func=AF.Silu),
                          [br], ['mixT'])
                    yield
                for d in range(4):
                    bk, br = bigmm(8 + d)
                    cp('act', tmpA[:, :], bk[:, :], [br], ['tmpA'])
                    yield
                    bk, br = bigmm(12 + d)
                    tt_('dve', uT[d][:, 2:514], bk[:, :], tmpA[:, :], ALU.mult, [br, 'tmpA'], ['uT%d' % d])
                    ur = 'uT%d' % d
                    cp('act', tmpB[:, :], uT[d][:, 0:512], [ur, 'wconv_c'], ['tmpB'], scale=wconv_c[:, d, 0:1])
                    stt(tmpB[:, :], uT[d][:, 1:513], wconv_c[:, d, 1:2], tmpB[:, :], ALU.mult, ALU.add,
                        [ur, 'tmpB'], ['tmpB'])
                    stt(tmpB[:, :], uT[d][:, 2:514], wconv_c[:, d, 2:3], tmpB[:, :], ALU.mult, ALU.add,
                        [ur, 'tmpB'], ['tmpB'])
                    if blk == 3:
                        cx.op('pool', lambda e, d=d: e.tensor_copy(out=cpst[:, d, :], in_=uT[d][:, 512:514]),
                              reads=[ur], writes=['cpst'])
                        if d == 3:
                            cx.dma('sp', conv_p.rearrange("p c k -> p (c k)"), cpst[:, :, :].rearrange("p c k -> p (c k)"),
                                   reads=['cpst'], key='d:o_convp')
                    else:
                        cx.op('pool', lambda e, d=d: e.tensor_copy(out=uT[d][:, 0:2], in_=uT[d][:, 512:514]),
                              [ur], [ur])
                    yield
                    bk, br = bigmm(16 + d)
                    cx.op('act', lambda e, bk=bk: e.activation(out=tmpC[:, :], in_=bk[:, :], func=AF.Silu),
                          [br], ['tmpC'])
                    yield
                    bk, br = bigmm(4 + d)
                    tt_('dve', tmpB[:, :], bk[:, :], tmpB[:, :], ALU.mult, [br, 'tmpB'], ['tmpB'])
                    tt_('pool', mixT[:, 4 + d, bs], tmpB[:, :], tmpC[:, :], ALU.mult, ['tmpB', 'tmpC'], ['mixT'])
                    yield

        def phase2():
            phase_now[0] = 2
            cx.op('pool', lambda e: e.memset(Vh[:, :, 64:65], 1.0), writes=['w_in_a', 'w_in_b', 'late', 'Vh'])
            cx.op('dve', lambda e: e.memset(KT[64:128, :], 0.0), reads=['late'], writes=['KT'])
            cx.op('dve', lambda e: e.memset(QT[64:128, :], 0.0), reads=['late'], writes=['QT'])
            wout_jobs = [(kc, hf) for kc in range(8) for hf in range(2)]
            wout_done = [0]

            def wout_some(k):
                for _ in range(k):
                    if wout_jobs:
                        kc, hf = wout_jobs.pop(0)
                        cx.dma('pool', w_out_b[:, kc, hf * 512:(hf + 1) * 512],
                               w_out[kc * 128:(kc + 1) * 128, hf * 512:(hf + 1) * 512],
                               reads=['late'], writes=['w_out_b'] if wout_done[0] == 0 else [],
                               key='d:w_out')
                        wout_done[0] += 1
                if not wout_jobs and 'w_out_b' in cx.res and not wout_state.get('final'):
                    cx.res['w_out_b'] = {'w': ('d:w_out', cx.dcnt['d:w_out']), 'r': {}}
                    wout_state['final'] = True
            cx.dma('sp', gpost_b, gpost, reads=['late'], writes=['gpost_b'])
            srot = [0]
            orot = [0]
            cx.op('dve', lambda e: e.memset(KT1[64:128, :], 0.0), reads=['late'], writes=['xst0'])
            cx.op('dve', lambda e: e.memset(QT1[64:128, :], 0.0), reads=['late'], writes=['xT_b'])
            cx.op('pool', lambda e: e.memset(Vh1[:, :, 64:65], 1.0), reads=['late', 'xT_b'], writes=['Vh1'])
            cx.res['QT1'] = cx.res['xT_b']
            QTs, KTs, Vhs = [QT, QT1], [KT, KT1], [Vh, Vh1]
            QTr, KTr, Vhr = ['QT', 'QT1'], ['KT', 'xst0'], ['Vh', 'Vh1']

            def produce(h):
                z = h % 2
                KTz, QTz, Vhz = KTs[z], QTs[z], Vhs[z]
                cx.op('pool', lambda e: e.tensor_copy(out=KTz[64:96, :], in_=kpeT_b[64:96, :]),
                      ['cT_b', 'late'], [KTr[z]])
                wout_some(4)
                for blk in range(4):
                    bs = slice(blk * 512, (blk + 1) * 512)
                    for kc in range(2):
                        mm(pb[4][:, :], w_ukv_b[:, kc, h * 128:h * 128 + 128], ckvT_b[:, kc, bs], kc == 0, kc == 1,
                           ['w_ukv_b', 'cT_b'], ['pb4'])
                    cp('dve', KTz[0:64, bs], pb[4][0:64, :], ['pb4'], [KTr[z]])
                    yield
                for half in range(2):
                    for k8 in range(8):
                        kt = half * 8 + k8
                        for kc in range(2):
                            mm(pb[4][:, k8 * 64:(k8 + 1) * 64], ckvT_b[:, kc, kt * 128:(kt + 1) * 128],
                               w_ukv_b[:, kc, h * 128 + 64:h * 128 + 128], kc == 0, kc == 1,
                               ['w_ukv_b', 'cT_b'], ['pb4'], signal=(k8 == 7 and kc == 1))
                    cp('act', Vhz[:, half * 8:(half + 1) * 8, 0:64], pb[4][:, :].rearrange("p (k v) -> p k v", v=64),
                       ['pb4'], [Vhr[z]])
                    yield
                for blk in range(4):
                    bs = slice(blk * 512, (blk + 1) * 512)
                    cx.dma('sp', ct_blk[:], r_ct[:, bs], reads=['late'], writes=['ct_blk'])
                    cx.dma('sp', st_blk[:], r_st[:, bs], reads=['late'], writes=['st_blk'])
                    for kc in range(3):
                        mm(pb[4][:, :], w_uq_b[:, kc, h * 96:h * 96 + 128], cqT_b[:, kc, bs], kc == 0, kc == 2,
                           ['w_uq_b', 'cT_b'], ['pb4'])
                    tt_('dve', tmpA[0:96, :], pb[4][0:96, :], ct_blk[:], ALU.mult, ['pb4', 'ct_blk'], ['tmpA'])
                    yield
                    for kc in range(3):
                        mm(pb[4][0:96, :], w_uqr_b[:, kc, h, :], cqT_b[:, kc, bs], kc == 0, kc == 2,
                           ['w_uqr_b', 'cT_b'], ['pb4'])
                    tt_('dve', tmpB[0:96, :], pb[4][0:96, :], st_blk[:], ALU.mult, ['pb4', 'st_blk'], ['tmpB'])
                    tt_('pool', QTz[0:96, bs], tmpA[0:96, :], tmpB[0:96, :], ALU.add, ['tmpA', 'tmpB'], [QTr[z]])
                    yield

            def attention(h):
                z = h % 2
                KTz, QTz, Vhz = KTs[z], QTs[z], Vhs[z]
                items = [(qb, kt) for qb in range(4) for kt in range((qb + 1) * 4)]
                obank = {}
                for qb in range(4):
                    obank[qb] = 2 + (orot[0] % 2)
                    orot[0] += 1
                base = srot[0]
                srot[0] += len(items)

                def emit_S(i):
                    qb, kt = items[i]
                    qlo = max(kt, qb * 4)
                    ncol = ((qb + 1) * 4 - qlo) * 128
                    sb = (base + i) % 2
                    pi = (base + i) % 3
                    sbr = 'pb%d' % sb
                    diag = kt >= qb * 4
                    mm(pb[sb][:, 0:ncol], KTz[:, kt * 128:(kt + 1) * 128], QTz[:, qlo * 128:(qb + 1) * 512],
                       True, not diag, [KTr[z], QTr[z]], [sbr])
                    if diag:
                        mm(pb[sb][:, 0:128], ident_b[:], maskneg[:], False, True, ['ident_b', 'maskneg'], [sbr])
                    cx.op('act', lambda e: e.activation(
                        out=pT[pi][:, 0:ncol], in_=pb[sb][:, 0:ncol], func=AF.Exp, scale=SCALE),
                        [sbr], ['pT%d' % pi])

                def emit_PV(i):
                    qb, kt = items[i]
                    qlo = max(kt, qb * 4)
                    pi = (base + i) % 3
                    ob = obank[qb]
                    obr = 'pb%d' % ob
                    if kt == 0:
                        mm(pb[ob][:, 0:260], ident_b[:], zeros260, True, False, ['ident_b', 'w_uqr_b'], [obr],
                           signal=False)
                    for qt in range(qlo, (qb + 1) * 4):
                        j = qt - qb * 4
                        c = (qt - qlo) * 128
                        last = (kt == (qb + 1) * 4 - 1)
                        mm(pb[ob][:, j * 65:(j + 1) * 65], pT[pi][:, c:c + 128], Vhz[:, kt, :],
                           False, last, ['pT%d' % pi, Vhr[z]], [obr],
                           signal=(qt == (qb + 1) * 4 - 1))
                    if kt == (qb + 1) * 4 - 1:
                        oview = pb[ob][:, 0:260].rearrange("p (j v) -> p j v", v=65)
                        col, rn = newstat()
                        rec = stat[:, scnt[0]:scnt[0] + 4]
                        scnt[0] += 4
                        cx.op('dve', lambda e: e.reciprocal(out=rec, in_=oview[:, :, 64]), [obr], [rn])
                        for j in range(4):
                            qt = qb * 4 + j
                            cp('dve' if j % 2 == 0 else 'act', opair[:, qt, (h % 2) * 64:(h % 2) * 64 + 64],
                               oview[:, j, 0:64], [obr, rn], ['opair'], scale=rec[:, j:j + 1])

                emit_S(0)
                for i in range(len(items)):
                    if i + 1 < len(items):
                        emit_S(i + 1)
                    emit_PV(i)
                    yield

            def otrans(h):
                hp = h // 2
                for blk in range(4):
                    bs = slice(blk * 512, (blk + 1) * 512)
                    for j in range(4):
                        tp(pb[4][:, j * 128:(j + 1) * 128], opair[:, blk * 4 + j, :], ['opair'], ['pb4'],
                           signal=(j == 3))
                    tt_('dve', mixT[:, hp, bs], pb[4][:, :], mixT[:, hp, bs], ALU.mult, ['pb4', 'mixT'], ['mixT'])
                    yield

            for _ in produce(0):
                yield
            for h in range(8):
                ga = attention(h)
                gp = produce(h + 1) if h + 1 < 8 else None
                k = 0
                for _ in ga:
                    yield
                    k += 1
                    if gp is not None and k % 2 == 0:
                        try:
                            next(gp)
                            yield
                        except StopIteration:
                            gp = None
                if gp is not None:
                    for _ in gp:
                        yield
                if h % 2 == 1:
                    for _ in otrans(h):
                        yield

        def out_tile(NT, mix_src, mix_res, xres, xres_res, out_ap, bA, bA_res, bB, bB_res, okey,
                     ysb_t=None, ysb_res='ysb'):
            if ysb_t is None:
                ysb_t = ysb
            for half, (bk, br) in enumerate(((bA, bA_res), (bB, bB_res))):
                for kc in range(8):
                    mm(bk[0:NT, :], mix_src(kc), w_out_b[:, kc, half * 512:(half + 1) * 512], kc == 0, kc == 7,
                       [mix_res, 'w_out_b'], [br])
            c1, r1 = sumsq(bA[0:NT, :], NT, [bA_res])
            c2, r2 = sumsq(bB[0:NT, :], NT, [bB_res])
            tt_('dve', c1[0:NT, :], c1[0:NT, :], c2[0:NT, :], ALU.add, [r1, r2], [r1])
            rstd_from(c1, r1, NT, D)
            for half, (bk, br) in enumerate(((bA, bA_res), (bB, bB_res))):
                hs = slice(half * 512, (half + 1) * 512)
                tt_('dve', ysb_t[0:NT, hs], bk[0:NT, :], gpost_b[0:NT, hs], ALU.mult, [br, 'gpost_b'], [ysb_res])
                stt(ysb_t[0:NT, hs], ysb_t[0:NT, hs], c1[0:NT, :], xres[0:NT, hs], ALU.mult, ALU.add,
                    [ysb_res, r1, xres_res], [ysb_res])
            cx.dma('sp', out_ap, ysb_t[0:NT, :], reads=[ysb_res], key=okey)

        def phase3():
            phase_now[0] = 3
            npair = 4 if sample_done[0] else 2
            ybufs = [(ysb, 'ysb'), (QT.bitcast(F32), 'QT'), (KT.bitcast(F32), 'KT')]
            cx.dma('sp', xst[0][:], xp[0:128, :], writes=['xst0'])
            for tt in range(NTT):
                s = tt % 2
                if tt + 1 < NTT:
                    cx.dma('sp', xst[1 - s][:], xp[(tt + 1) * 128:(tt + 2) * 128, :], writes=['xst%d' % (1 - s)])
                bp = tt % npair
                yb, yr = ybufs[tt % 3]
                out_tile(128, lambda kc, tt=tt: mixT[:, kc, tt * 128:(tt + 1) * 128], 'mixT',
                         xst[s], 'xst%d' % s, y_p[tt * 128:(tt + 1) * 128, :],
                         pb[2 * bp], 'pb%d' % (2 * bp), pb[2 * bp + 1], 'pb%d' % (2 * bp + 1), 'd:o_' + yr,
                         ysb_t=yb, ysb_res=yr)
                yield

        def sample():
            P5, P6, P7 = pb[5], pb[6], pb[7]
            xs_t = xst[1]
            cx.dma('sp', xs_t[0:16, :], xs, writes=['xst1'])
            front_tile(16, xs_t, 'xst1', xs_t, 'xst1', lambda kc: xTs[:, kc, :], 'xTs', [P5, P6], ['pb5', 'pb6'])
            small_proj(16, lambda kc: xTs[:, kc, :], 'xTs', P5, 'pb5', P6, 'pb6',
                       cos_s[:, :], sinm_s[:, :], ['cos_s', 'sinm_s'],
                       ckvn[0], 'ckvn0', kst[0], 'kst', ckv_s[:, :], kpe_s[:, :],
                       cqTs[:, :, :], ckvTs[:, :, :], kpeTs[:, :], 'cTs', [P5, P6], ['pb5', 'pb6'], 0)
            cp('pool', ckvn_sb[:, :], ckvn[0][0:16, :], ['ckvn0'], ['ckvn_sb'])
            cx.op('pool', lambda e: e.memset(kst[0][:, 0:64], 0.0), ['kst'], ['kst'])
            yield
            for h in range(8):
                for kc in range(3):
                    mm(P5[0:64, 0:16], w_uq_b[:, kc, h * 96:h * 96 + 64], cqTs[:, kc, :], kc == 0, kc == 2,
                       ['w_uq_b', 'cTs'], ['pb5'])
                for kc in range(3):
                    mm(P5[0:32, 16:32], w_uq_b[:, kc, h * 96 + 64:h * 96 + 96], cqTs[:, kc, :], kc == 0, kc == 2,
                       ['w_uq_b', 'cTs'], ['pb5'])
                for kc in range(3):
                    mm(P5[0:32, 32:48], w_uqr_b[:, kc, h, 64:96], cqTs[:, kc, :], kc == 0, kc == 2,
                       ['w_uqr_b', 'cTs'], ['pb5'])
                cp('act', qnT_s[:, :], P5[0:64, 0:16], ['pb5'], ['qnT_s'])
                tt_('dve', tmpD[0:32, 112:128], P5[0:32, 16:32], ct_s[:, :], ALU.mult, ['pb5', 'ct_s'], ['tmpD'])
                tt_('dve', tmpD[0:32, 128:144], P5[0:32, 32:48], st_s[:, :], ALU.mult, ['pb5', 'st_s'], ['tmpD'])
                tt_('dve', qpeT[0:32, :, h, :], tmpD[0:32, 112:128].rearrange("p (b i) -> p b i", b=4),
                    tmpD[0:32, 128:144].rearrange("p (b i) -> p b i", b=4), ALU.add, ['tmpD'], ['qpeT'])
                for kc in range(2):
                    mm(P6[:, kc * 16:(kc + 1) * 16], w_ukT_b[:, h, kc * 128:(kc + 1) * 128], qnT_s[:, :], True, True,
                       ['w_ukT_b', 'qnT_s'], ['pb6'])
                cp('dve', qlatT[:, :, :, h, :], P6[:, 0:32].rearrange("p (k b i) -> p k b i", k=2, b=4),
                   ['pb6'], ['qlatT'])
                yield
            groups = [(b, g) for b in range(4) for g in range(16)]
            NG = len(groups)

            def stage_A(n):
                b, g = groups[n]
                s3 = n % 3
                cx.dma('pool', gck[s3][:, :], cckv, reads=['idx_c'], writes=['gck%d' % s3],
                       indirect=bass.IndirectOffsetOnAxis(ap=idx_c[:, b, g:g + 1], axis=0))
                if g % 8 == 0:
                    cx.dma('pool', gkp[:, :], ckpe, reads=['idx_k'], writes=['gkp'],
                           indirect=bass.IndirectOffsetOnAxis(ap=idx_k[:, b, g // 8:g // 8 + 1], axis=0))

            def stage_B(n):
                b, g = groups[n]
                s3 = n % 3
                s = n % 2
                for r in range(4):
                    for tl in range(2):
                        t = r * 2 + tl
                        for kc in range(2):
                            mm(P5[:, (tl * 2 + kc) * 128:(tl * 2 + kc + 1) * 128],
                               gck[s3][:, t * 256 + kc * 128:t * 256 + (kc + 1) * 128], ident_b[:], True, True,
                               ['gck%d' % s3, 'ident_b'], ['pb5'], signal=(tl == 1 and kc == 1))
                    cp('dve' if r % 2 == 0 else 'act',
                       ckTs[s][:, :, r * 2:r * 2 + 2, :].rearrange("p k t n -> p t k n"),
                       P5[:, :].rearrange("p (t k n) -> p t k n", t=2, k=2), ['pb5'], ['ckTs%d' % s])
                    tk0 = (g % 8) * 8 + r * 2
                    mm(P6[0:64, 256:384], gkp[:, tk0 * 32:(tk0 + 2) * 32], ident_b[:], True, True,
                       ['gkp', 'ident_b'], ['pb6'])
                    cp('act' if r % 2 == 0 else 'dve', kpTs[s][:, r, :], P6[0:64, 256:384], ['pb6'], ['kpTs%d' % s])
                    yield

            def stage_C1(n):
                b, g = groups[n]
                s = n % 2
                for t in range(8):
                    o = pb[6][:, t * 32:(t + 1) * 32]
                    for kc in range(2):
                        mm(o, ckTs[s][:, kc, t, :], qlatT[:, kc, b, :, :].rearrange("p h i -> p (h i)"),
                           kc == 0, False, ['ckTs%d' % s, 'qlatT'], ['pb6'], signal=False)
                    mm(o, kpTs[s][(t % 2) * 32:(t % 2) * 32 + 32, t // 2, :],
                       qpeT[(t % 2) * 32:(t % 2) * 32 + 32, b, :, :].rearrange("p h i -> p (h i)"), False, True,
                       ['kpTs%d' % s, 'qpeT'], ['pb6'], signal=(t == 7))
                cx.op('act', lambda e, s=s: e.activation(out=pTs[s][:, :], in_=pb[6][:, 0:256], func=AF.Exp,
                                                         scale=SCALE), ['pb6'], ['pTs%d' % s])

            def stage_C2(n):
                b, g = groups[n]
                s = n % 2
                s3 = n % 3
                if g == 0:
                    mm(pb[7][0:32, 0:260], ident_b[:, 0:32], zeros260, True, False, ['ident_b', 'w_uqr_b'], ['pb7'],
                       signal=False)
                for t in range(8):
                    mm(pb[7][0:32, 0:256], pTs[s][:, t * 32:(t + 1) * 32], gck[s3][:, t * 256:(t + 1) * 256],
                       False, False, ['pTs%d' % s, 'gck%d' % s3], ['pb7'], signal=False)
                for t in range(8):
                    mm(pb[7][0:32, 256:257], pTs[s][:, t * 32:(t + 1) * 32], ones_b[:, 0:1],
                       False, False, ['pTs%d' % s, 'ones_b'], ['pb7'], signal=(t == 7))

            def batch_tail(b):
                o = pb[6][0:16, 0:32]
                for kc in range(2):
                    mm(o, ckvTs[:, kc, :], qlatT[:, kc, b, :, :].rearrange("p h i -> p (h i)"), kc == 0, False,
                       ['cTs', 'qlatT'], ['pb6'], signal=False)
                mm(o, kpeTs[:, :], qpeT[0:32, b, :, :].rearrange("p h i -> p (h i)"), False, True,
                   ['cTs', 'qpeT'], ['pb6'])
                cx.op('act', lambda e: e.activation(out=tmpD[0:16, 144:176], in_=pb[6][0:16, 0:32], func=AF.Exp,
                                                    scale=SCALE), ['pb6'], ['tmpD'])
                tt_('dve', pTn[:, :], tmpD[0:16, 144:176], sm_t[:, b, :], ALU.mult, ['tmpD', 'sm_t'], ['pTn'])
                mm(pb[7][0:32, 0:256], pTn[:, :], ckvn_sb[:, :], False, False, ['pTn', 'ckvn_sb'], ['pb7'], signal=False)
                mm(pb[7][0:32, 256:257], pTn[:, :], ones_b[0:16, 0:1], False, True, ['pTn', 'ones_b'], ['pb7'])
                cp('dve', ssm[:, 0:1], pb[7][0:32, 256:257], ['pb7'], ['ssm'])
                cx.op('dve', lambda e: e.reciprocal(out=ssm[:, 1:2], in_=ssm[:, 0:1]), ['ssm'], ['ssm'])
                cp('dve', olat[:, :], pb[7][0:32, 0:256], ['pb7', 'ssm'], ['olat'], scale=ssm[:, 1:2])
                for kc in range(2):
                    tp(P5[:, kc * 32:(kc + 1) * 32], olat[:, kc * 128:(kc + 1) * 128], ['olat'], ['pb5'], signal=(kc == 1))
                cp('dve', olatT[:, :, :, b * 4:(b + 1) * 4], P5[:, 0:64].rearrange("p (k h i) -> p k h i", k=2, h=8),
                   ['pb5'], ['olatT'])

            def feature_major():
                def smm(j, bank, col):
                    for kc in range(8):
                        mm(bank[:, col:col + 16], w_in_b[:, kc, 672 + j * 128:672 + (j + 1) * 128], xTs[:, kc, :],
                           kc == 0, kc == 7, ['w_in_b', 'xTs'], ['pb5'])
                for d in range(4):
                    smm(d, P5, 0)
                    cx.op('act', lambda e, d=d: e.activation(out=mixTs[:, d, :], in_=P5[:, 0:16], func=AF.Silu),
                          ['pb5'], ['mixTs'])
                yield
                for d in range(4):
                    smm(8 + d, P5, 0)
                    smm(12 + d, P5, 16)
                    smm(16 + d, P5, 32)
                    smm(4 + d, P5, 48)
                    cp('act', tmpD[:, 64:80], P5[:, 0:16], ['pb5'], ['tmpD'])
                    uv = uS[:, d, :, :]
                    tt_('dve', uv[:, :, 2:6], P5[:, 16:32].rearrange("p (b i) -> p b i", b=4),
                        tmpD[:, 64:80].rearrange("p (b i) -> p b i", b=4), ALU.mult, ['pb5', 'tmpD'], ['uS'])
                    yv = tmpD[:, 80:96].rearrange("p (b i) -> p b i", b=4)
                    ts_('dve', yv, uv[:, :, 0:4], wconv_c[:, d, 0:1], None, ALU.mult, None, ['uS', 'wconv_c'], ['tmpD'])
                    stt(yv, uv[:, :, 1:5], wconv_c[:, d, 1:2], yv, ALU.mult, ALU.add, ['uS', 'tmpD'], ['tmpD'])
                    stt(yv, uv[:, :, 2:6], wconv_c[:, d, 2:3], yv, ALU.mult, ALU.add, ['uS', 'tmpD'], ['tmpD'])
                    cx.op('act', lambda e: e.activation(out=tmpD[:, 96:112], in_=P5[:, 32:48], func=AF.Silu),
                          ['pb5'], ['tmpD'])
                    tt_('dve', tmpD[:, 80:96], P5[:, 48:64], tmpD[:, 80:96], ALU.mult, ['pb5', 'tmpD'], ['tmpD'])
                    tt_('dve', mixTs[:, 4 + d, :], tmpD[:, 80:96], tmpD[:, 96:112], ALU.mult, ['tmpD'], ['mixTs'])
                    yield
                cx.op('dve', lambda e: e.tensor_copy(out=tmpD[:, 224:256].rearrange("p (c b k) -> p c b k", c=4, b=4),
                                                     in_=uS[:, :, :, 4:6]), reads=['uS'], writes=['tmpD'])
                cx.dma('sp', conv_s.rearrange("p c b k -> p (c b k)"), tmpD[:, 224:256], reads=['tmpD'], key='d:o_convs')

            cx.dma('sp', qpeT[32:64, :, :, :].rearrange("p b h i -> p (b h i)"),
                   qpeT[0:32, :, :, :].rearrange("p b h i -> p (b h i)"), reads=['qpeT'], writes=['qpeT'], key='d:qpeT2')
            stage_A(0)
            stage_A(1)
            for _ in stage_B(0):
                yield
            for _ in feature_major():
                yield
            for n in range(NG):
                stage_C1(n)
                yield
                if n + 1 < NG:
                    for _ in stage_B(n + 1):
                        yield
                if n + 2 < NG:
                    stage_A(n + 2)
                stage_C2(n)
                if groups[n][1] == 15:
                    batch_tail(groups[n][0])
                yield
            for hp in range(4):
                n = 0
                for hh in range(2):
                    h = hp * 2 + hh
                    for kc in range(2):
                        mm(P5[:, 0:16], w_uvp_b[:, kc, h, :], olatT[:, kc, h, :], n == 0, n == 3,
                           ['w_uvp_b', 'olatT'], ['pb5'])
                        n += 1
                tt_('dve', mixTs[:, hp, :], P5[:, 0:16], mixTs[:, hp, :], ALU.mult, ['pb5', 'mixTs'], ['mixTs'])
            yield
            assert wout_state.get('final'), 'w_out not fully issued before the sample epilogue'
            cx.dma('sp', xs_t[0:16, :], xs, writes=['xst1'])
            out_tile(16, lambda kc: mixTs[:, kc, :], 'mixTs', xs_t, 'xst1', y_s[:, :], P5, 'pb5', P6, 'pb6', 'd:o_ysb')
            sample_done[0] = True
            yield

        for _ in prologue():
            pass

        def chain(*gens):
            for g_ in gens:
                for _ in g_:
                    yield

        P = chain(phase1(), phase2(), phase3())
        S = sample()
        next(S)
        p_alive, s_alive = True, True
        acc = 0.0
        RATIOS = {1: 0.85, 2: 0.85, 3: 1.0}
        while p_alive or s_alive:
            if p_alive:
                try:
                    next(P)
                except StopIteration:
                    p_alive = False
            acc += RATIOS[phase_now[0]]
            while s_alive and (acc >= 1.0 or not p_alive):
                acc -= 1.0
                try:
                    next(S)
                except StopIteration:
                    s_alive = False
        cx.finish('sp')
        build_nc.stats = dict(nwaits=cx.nwaits, cnt=dict(cx.cnt), nsem=len(cx.dsem) + 5, late=off[0])
    return nc


_NC = None


def _rope_tables():
    inv = (np.float32(10000.0) ** (-(np.arange(0, 32, 2, dtype=np.float32)) / np.float32(32))).astype(np.float32)

    def cs(pos):
        ang = (pos.astype(np.float32)[:, None] * inv[None, :]).astype(np.float32)
        ang = np.concatenate([ang, ang], axis=-1).astype(np.float64)
        return np.cos(ang).astype(np.float32), np.sin(ang).astype(np.float32)

    cp_, sp_ = cs(np.arange(SEQ))
    sgn = np.concatenate([-np.ones(16, np.float32), np.ones(16, np.float32)])
    r_cos = np.ascontiguousarray(cp_.reshape(NTT, 128, 32).transpose(1, 0, 2))
    r_sinm = np.ascontiguousarray((sp_ * sgn).reshape(NTT, 128, 32).transpose(1, 0, 2))
    r_ct = np.ones((96, SEQ), np.float32)
    r_st = np.zeros((96, SEQ), np.float32)
    r_ct[64:96] = cp_.T
    r_st[64:96] = sp_.T
    pos_s = PAST + (np.arange(16) % 4)
    cs_, ss_ = cs(pos_s)
    smask = np.zeros((16, 4, 32), np.float32)
    for b in range(4):
        for k in range(4):
            for h in range(8):
                for i in range(4):
                    if k <= i:
                        smask[b * 4 + k, b, h * 4 + i] = 1.0
    return dict(r_cos=r_cos, r_sinm=r_sinm, r_ct=r_ct, r_st=r_st,
                rs_cos=cs_, rs_sinm=(ss_ * sgn).astype(np.float32),
                rs_ct=np.ascontiguousarray(cs_.T), rs_st=np.ascontiguousarray(ss_.T), smask=smask)


def kernel(x_prompt, x_sample, cache_ckv, cache_kpe, state_conv, page_table,
           g_pre, w_in, g_qnorm, w_uq, g_kvnorm, w_ukv, w_conv, w_out, g_post):
    global _NC
    if _NC is None:
        _NC = build_nc()
    nc = _NC
    f = lambda a: np.ascontiguousarray(np.asarray(a, dtype=np.float32))
    x_prompt, x_sample = f(x_prompt), f(x_sample)
    tabs = _rope_tables()
    cckv = f(cache_ckv).reshape(NPHYS * 16, 2048)
    ckpe = f(cache_kpe).reshape(NPHYS * 2, 2048)
    pt = np.asarray(page_table, dtype=np.int32)
    sc = f(state_conv)[0]
    shared = dict(
        cckv=cckv, ckpe=ckpe,
        gpre=np.ascontiguousarray(f(g_pre)[0].reshape(8, 128).T),
        w_in=f(w_in)[0],
        gq=np.ascontiguousarray(f(g_qnorm)[0].reshape(3, 128).T),
        w_uq=f(w_uq)[0],
        gkv=np.ascontiguousarray(np.broadcast_to(f(g_kvnorm)[0][None, :], (128, 256))),
        w_ukv=f(w_ukv)[0],
        wconv=np.ascontiguousarray(f(w_conv)[0].reshape(3, 4, 128).transpose(2, 1, 0)),
        w_out=f(w_out)[0],
        gpost=np.ascontiguousarray(np.broadcast_to(f(g_post)[0][None, :], (128, D))),
        **tabs)
    in_maps = []
    for c in range(8):
        m = dict(shared)
        m["xp"] = x_prompt[c]
        m["xs"] = np.ascontiguousarray(x_sample[4 * c:4 * c + 4].reshape(16, D))
        m["ptab"] = np.ascontiguousarray(pt[4 * c:4 * c + 4].T)
        scc = sc[4 * c:4 * c + 4]
        m["sconv"] = np.ascontiguousarray(scc.reshape(4, 2, 4, 128).transpose(3, 2, 0, 1))
        in_maps.append(m)
    res = run_bass_kernel_spmd(nc, in_maps, core_ids=list(range(8)))
    R = res.results
    g = lambda k, c: np.asarray(R[c][k], dtype=np.float32)
    y_prompt = np.stack([g("y_p", c) for c in range(8)])
    y_sample = np.concatenate([g("y_s", c).reshape(4, 4, D) for c in range(8)])
    ckv_p = np.stack([g("ckv_p", c) for c in range(8)])[None]
    kpe_p = np.stack([g("kpe_p", c) for c in range(8)])[None]
    conv_p = np.stack([g("conv_p", c).transpose(2, 1, 0).reshape(2, 512) for c in range(8)])[None]
    ckv_s = np.concatenate([g("ckv_s", c).reshape(4, 4, 256) for c in range(8)])[None]
    kpe_s = np.concatenate([g("kpe_s", c).reshape(4, 4, 32) for c in range(8)])[None]
    conv_s = np.concatenate([g("conv_s", c).transpose(2, 3, 1, 0).reshape(4, 2, 512) for c in range(8)])[None]
    return (y_prompt, y_sample, ckv_p, kpe_p, conv_p, ckv_s, kpe_s, conv_s)
```
